# Optimizing a Trainium2 kernel written in Bass

```python
import math
import jax, jax.numpy as jnp
from jax import lax
import numpy as np

D_MODEL = 1024
BATCH = 8
SEQ = 4096
DEPTH = 1

HEAD_DIM = 64
ROPE_DIM = HEAD_DIM // 4
ROPE_THETA = 500000.0
DIFF_HEADS = 4
DIFF_V_DIM = 2 * HEAD_DIM
NSA_HEADS = 8
NSA_KV_GROUPS = 2
NSA_GQA = NSA_HEADS // NSA_KV_GROUPS
CMP_BLOCK = 32
CMP_STRIDE = 16
CMP_HIDDEN = 2 * HEAD_DIM
SLC_BLOCK = 64
SLC_TOPK = 16
WINDOW = 512
D_MIX = DIFF_HEADS * DIFF_V_DIM + NSA_HEADS * HEAD_DIM
D_IN = (2 * DIFF_HEADS * 2 * HEAD_DIM + DIFF_HEADS * DIFF_V_DIM
        + NSA_HEADS * HEAD_DIM + 6 * NSA_KV_GROUPS * HEAD_DIM + 3 * NSA_HEADS)
D_FF = 2816
CONV_WIDTH = 3
Q_BLOCK = 128
LN_EPS = 1e-5
RMS_EPS = 1e-5
NEG_INF = -1e30
FORCED_SCORE = 1e6
DEEPNORM_ALPHA = (2 * DEPTH) ** 0.25
DEEPNORM_BETA = (8 * DEPTH) ** -0.25

kernel_name = "hymba_diff_nsa_convglu_deepnorm"

f32 = jnp.float32


def layer_norm(x, g, b):
    xf = x.astype(f32)
    mu = jnp.mean(xf, axis=-1, keepdims=True)
    var = jnp.mean(jnp.square(xf - mu), axis=-1, keepdims=True)
    return ((xf - mu) * lax.rsqrt(var + LN_EPS) * g.astype(f32) + b.astype(f32)).astype(x.dtype)


def rope_partial(x, pos):
    half = ROPE_DIM // 2
    inv_freq = 1.0 / (ROPE_THETA ** (jnp.arange(half, dtype=f32) / half))
    ang = pos.astype(f32)[..., None] * inv_freq
    ang = ang.reshape(ang.shape[:2] + (1,) * (x.ndim - 3) + (half,))
    cos = jnp.cos(ang).astype(x.dtype)
    sin = jnp.sin(ang).astype(x.dtype)
    x1 = x[..., :half]
    x2 = x[..., half:ROPE_DIM]
    return jnp.concatenate([x1 * cos - x2 * sin, x2 * cos + x1 * sin, x[..., ROPE_DIM:]], axis=-1)


def compress(x, pe, w1, b1, w2):
    B, S, G, d = x.shape
    n_chunk = S // CMP_STRIDE
    r = CMP_BLOCK // CMP_STRIDE
    n_cmp = n_chunk - r + 1
    chunks = x.reshape(B, n_chunk, CMP_STRIDE, G, d)
    blocks = jnp.concatenate([chunks[:, i:i + n_cmp] for i in range(r)], axis=2)
    blocks = blocks + pe[None, None, :, None, :]
    flat = blocks.transpose(0, 1, 3, 2, 4).reshape(B, n_cmp, G, CMP_BLOCK * d)
    hid = jax.nn.gelu(flat @ w1 + b1)
    return hid @ w2


def cmp_to_slc_overlap(n_cmp, n_slc):
    cs = np.arange(n_cmp)[:, None] * CMP_STRIDE
    ss = np.arange(n_slc)[None, :] * SLC_BLOCK
    ov = np.clip(np.minimum(cs + CMP_BLOCK, ss + SLC_BLOCK) - np.maximum(cs, ss), 0, None)
    return jnp.asarray(ov / CMP_BLOCK, dtype=f32)


def masked_softmax(s, mask):
    s = jnp.where(mask, s.astype(f32), NEG_INF)
    return jnp.where(mask, jax.nn.softmax(s, axis=-1), 0.0)


def hybrid_mixer(h, positions, w_in, lq1, lk1, lq2, lk2, diff_g, lam_init,
                 pe_k, w1_k, b1_k, w2_k, pe_v, w1_v, b1_v, w2_v):
    B, S, _ = h.shape
    G, R, d = NSA_KV_GROUPS, NSA_GQA, HEAD_DIM
    scale = HEAD_DIM ** -0.5
    proj = h @ w_in
    sizes = [DIFF_HEADS * 2 * d, DIFF_HEADS * 2 * d, DIFF_HEADS * DIFF_V_DIM, NSA_HEADS * d] + [G * d] * 6 + [NSA_HEADS * 3]
    splits = np.cumsum(sizes)[:-1].tolist()
    q_d, k_d, v_d, q_n, k_c, v_c, k_s, v_s, k_w, v_w, g_n = jnp.split(proj, splits, axis=-1)

    q_d = rope_partial(q_d.reshape(B, S, DIFF_HEADS, 2, d), positions)
    k_d = rope_partial(k_d.reshape(B, S, DIFF_HEADS, 2, d), positions)
    v_d = v_d.reshape(B, S, DIFF_HEADS, DIFF_V_DIM)
    lam = (jnp.exp(jnp.sum(lq1.astype(f32) * lk1.astype(f32)))
           - jnp.exp(jnp.sum(lq2.astype(f32) * lk2.astype(f32))) + lam_init)

    q_n = rope_partial(q_n.reshape(B, S, G, R, d), positions)
    r_c = CMP_BLOCK // CMP_STRIDE
    n_cmp = S // CMP_STRIDE - r_c + 1
    cmp_pos = positions[:, CMP_BLOCK - 1::CMP_STRIDE][:, :n_cmp]
    k_cmp = rope_partial(compress(k_c.reshape(B, S, G, d), pe_k, w1_k, b1_k, w2_k), cmp_pos)
    v_cmp = compress(v_c.reshape(B, S, G, d), pe_v, w1_v, b1_v, w2_v)
    cmp_end = jnp.arange(n_cmp) * CMP_STRIDE + CMP_BLOCK - 1
    n_slc = S // SLC_BLOCK
    top_k = min(SLC_TOPK, n_slc)
    overlap = cmp_to_slc_overlap(n_cmp, n_slc)
    k_sel = rope_partial(k_s.reshape(B, S, G, d), positions)
    kb = k_sel.reshape(B, n_slc, SLC_BLOCK, G, d).transpose(0, 3, 1, 2, 4)
    vb = v_s.reshape(B, n_slc, SLC_BLOCK, G, d).transpose(0, 3, 1, 2, 4)
    pad = ((0, 0), (WINDOW, 0), (0, 0), (0, 0))
    k_win = jnp.pad(rope_partial(k_w.reshape(B, S, G, d), positions), pad)
    v_win = jnp.pad(v_w.reshape(B, S, G, d), pad)
    gates = jax.nn.sigmoid(g_n).reshape(B, S, G, R, 3)
    bi = jnp.arange(B)[:, None, None, None]
    gi = jnp.arange(G)[None, :, None, None]
    blk = jnp.arange(n_slc)
    key_idx = jnp.arange(S)

    def block_fn(i):
        s0 = i * Q_BLOCK
        t = s0 + jnp.arange(Q_BLOCK)
        qd = lax.dynamic_slice_in_dim(q_d, s0, Q_BLOCK, axis=1)
        s = jnp.einsum('bqhcd,bkhcd->bhcqk', qd, k_d) * scale
        p = masked_softmax(s, t[:, None] >= key_idx[None, :])
        a = p[:, :, 0] - lam * p[:, :, 1]
        od = jnp.einsum('bhqk,bkhe->bqhe', a.astype(v_d.dtype), v_d).astype(f32)
        od = od * lax.rsqrt(jnp.mean(od * od, axis=-1, keepdims=True) + RMS_EPS)
        od = (od * diff_g.astype(f32) * (1.0 - lam_init)).astype(h.dtype).reshape(B, Q_BLOCK, DIFF_HEADS * DIFF_V_DIM)

        qn = lax.dynamic_slice_in_dim(q_n, s0, Q_BLOCK, axis=1)
        s_c = jnp.einsum('bqgrd,bngd->bgrqn', qn, k_cmp) * scale
        p_c = masked_softmax(s_c, cmp_end[None, :] <= t[:, None])
        o_c = jnp.einsum('bgrqn,bngd->bqgrd', p_c.astype(v_cmp.dtype), v_cmp)

        imp = jnp.sum(p_c, axis=2) @ overlap
        cur = t // SLC_BLOCK
        forced = (blk[None, :] == 0) | (blk[None, :] == cur[:, None]) | (blk[None, :] == cur[:, None] - 1)
        imp = jnp.where(blk[None, :] > cur[:, None], -1.0, imp)
        imp = jnp.where(forced, FORCED_SCORE, imp)
        _, idx = lax.top_k(imp, top_k)
        ks = kb[bi, gi, idx].reshape(B, G, Q_BLOCK, top_k * SLC_BLOCK, d)
        vs = vb[bi, gi, idx].reshape(B, G, Q_BLOCK, top_k * SLC_BLOCK, d)
        kpos = (idx[..., None] * SLC_BLOCK + jnp.arange(SLC_BLOCK)).reshape(B, G, Q_BLOCK, top_k * SLC_BLOCK)
        s_s = jnp.einsum('bqgrd,bgqnd->bgrqn', qn, ks) * scale
        p_s = masked_softmax(s_s, (kpos <= t[None, None, :, None])[:, :, None])
        o_s = jnp.einsum('bgrqn,bgqnd->bqgrd', p_s.astype(vs.dtype), vs)

        kw = lax.dynamic_slice_in_dim(k_win, s0, WINDOW + Q_BLOCK, axis=1)
        vw = lax.dynamic_slice_in_dim(v_win, s0, WINDOW + Q_BLOCK, axis=1)
        wpos = s0 - WINDOW + jnp.arange(WINDOW + Q_BLOCK)
        wmask = (wpos[None, :] >= 0) & (wpos[None, :] <= t[:, None]) & (wpos[None, :] > t[:, None] - WINDOW)
        s_w = jnp.einsum('bqgrd,bkgd->bgrqk', qn, kw) * scale
        p_w = masked_softmax(s_w, wmask)
        o_w = jnp.einsum('bgrqk,bkgd->bqgrd', p_w.astype(vw.dtype), vw)

        gb = lax.dynamic_slice_in_dim(gates, s0, Q_BLOCK, axis=1)
        on = gb[..., 0:1] * o_c + gb[..., 1:2] * o_s + gb[..., 2:3] * o_w
        on = on.astype(h.dtype).reshape(B, Q_BLOCK, NSA_HEADS * d)
        return jnp.concatenate([od, on], axis=-1)

    out = lax.map(block_fn, jnp.arange(S // Q_BLOCK))
    return out.transpose(1, 0, 2, 3).reshape(B, S, D_MIX)


def conv_glu_ffn(h, w_up, conv_w, conv_b, w_down):
    S = h.shape[1]
    u = h @ w_up
    up = jnp.pad(u, ((0, 0), (CONV_WIDTH - 1, 0), (0, 0)))
    u = sum(conv_w[k] * up[:, k:k + S] for k in range(CONV_WIDTH)) + conv_b
    gate, val = jnp.split(u, 2, axis=-1)
    return (jax.nn.silu(gate) * val) @ w_down


def setup_inputs(seed: int = 0) -> dict:
    key = jax.random.key(seed)
    ks = jax.random.split(key, 26)
    L, d, G = DEPTH, HEAD_DIM, NSA_KV_GROUPS
    beta = DEEPNORM_BETA
    nrm = lambda k, shp: jax.random.normal(k, shp, f32)
    x = nrm(ks[0], (BATCH, SEQ, D_MODEL))
    positions = jnp.tile(jnp.arange(SEQ, dtype=jnp.int32)[None, :], (BATCH, 1))
    col_scale = jnp.concatenate([
        jnp.ones(2 * DIFF_HEADS * 2 * d, f32), jnp.full(DIFF_HEADS * DIFF_V_DIM, beta, f32),
        jnp.ones(NSA_HEADS * d, f32),
        jnp.tile(jnp.concatenate([jnp.ones(G * d, f32), jnp.full(G * d, beta, f32)]), 3),
        jnp.ones(3 * NSA_HEADS, f32)])
    w_in = nrm(ks[1], (L, D_MODEL, D_IN)) * (D_MODEL ** -0.5) * col_scale
    lambda_q1 = 0.1 * nrm(ks[2], (L, d))
    lambda_k1 = 0.1 * nrm(ks[3], (L, d))
    lambda_q2 = 0.1 * nrm(ks[4], (L, d))
    lambda_k2 = 0.1 * nrm(ks[5], (L, d))
    diff_norm_g = 1.0 + 0.02 * nrm(ks[6], (L, DIFF_V_DIM))
    cmp_pe_k = 0.02 * nrm(ks[7], (L, CMP_BLOCK, d))
    cmp_w1_k = nrm(ks[8], (L, CMP_BLOCK * d, CMP_HIDDEN)) * (CMP_BLOCK * d) ** -0.5
    cmp_b1_k = 0.02 * nrm(ks[9], (L, CMP_HIDDEN))
    cmp_w2_k = nrm(ks[10], (L, CMP_HIDDEN, d)) * CMP_HIDDEN ** -0.5
    cmp_pe_v = 0.02 * nrm(ks[11], (L, CMP_BLOCK, d))
    cmp_w1_v = nrm(ks[12], (L, CMP_BLOCK * d, CMP_HIDDEN)) * (CMP_BLOCK * d) ** -0.5
    cmp_b1_v = 0.02 * nrm(ks[13], (L, CMP_HIDDEN))
    cmp_w2_v = nrm(ks[14], (L, CMP_HIDDEN, d)) * CMP_HIDDEN ** -0.5
    w_out = nrm(ks[15], (L, D_MIX, D_MODEL)) * (D_MIX ** -0.5) * beta
    ln1_g = 1.0 + 0.02 * nrm(ks[16], (L, D_MODEL))
    ln1_b = 0.02 * nrm(ks[17], (L, D_MODEL))
    w_up = nrm(ks[18], (L, D_MODEL, 2 * D_FF)) * (D_MODEL ** -0.5) * beta
    conv_w = nrm(ks[19], (L, CONV_WIDTH, 2 * D_FF)) * CONV_WIDTH ** -0.5
    conv_b = 0.02 * nrm(ks[20], (L, 2 * D_FF))
    w_down = nrm(ks[21], (L, D_FF, D_MODEL)) * (D_FF ** -0.5) * beta
    ln2_g = 1.0 + 0.02 * nrm(ks[22], (L, D_MODEL))
    ln2_b = 0.02 * nrm(ks[23], (L, D_MODEL))
    return {"x": x, "positions": positions, "w_in": w_in,
            "lambda_q1": lambda_q1, "lambda_k1": lambda_k1, "lambda_q2": lambda_q2, "lambda_k2": lambda_k2,
            "diff_norm_g": diff_norm_g,
            "cmp_pe_k": cmp_pe_k, "cmp_w1_k": cmp_w1_k, "cmp_b1_k": cmp_b1_k, "cmp_w2_k": cmp_w2_k,
            "cmp_pe_v": cmp_pe_v, "cmp_w1_v": cmp_w1_v, "cmp_b1_v": cmp_b1_v, "cmp_w2_v": cmp_w2_v,
            "w_out": w_out, "ln1_g": ln1_g, "ln1_b": ln1_b,
            "w_up": w_up, "conv_w": conv_w, "conv_b": conv_b, "w_down": w_down,
            "ln2_g": ln2_g, "ln2_b": ln2_b}


def reference(x, positions, w_in, lambda_q1, lambda_k1, lambda_q2, lambda_k2, diff_norm_g,
              cmp_pe_k, cmp_w1_k, cmp_b1_k, cmp_w2_k, cmp_pe_v, cmp_w1_v, cmp_b1_v, cmp_w2_v,
              w_out, ln1_g, ln1_b, w_up, conv_w, conv_b, w_down, ln2_g, ln2_b):
    h = x
    for l in range(DEPTH):
        lam_init = 0.8 - 0.6 * math.exp(-0.3 * l)
        mix = hybrid_mixer(h, positions, w_in[l], lambda_q1[l], lambda_k1[l], lambda_q2[l], lambda_k2[l],
                           diff_norm_g[l], lam_init,
                           cmp_pe_k[l], cmp_w1_k[l], cmp_b1_k[l], cmp_w2_k[l],
                           cmp_pe_v[l], cmp_w1_v[l], cmp_b1_v[l], cmp_w2_v[l])
        h = layer_norm(DEEPNORM_ALPHA * h + mix @ w_out[l], ln1_g[l], ln1_b[l])
        h = layer_norm(DEEPNORM_ALPHA * h + conv_glu_ffn(h, w_up[l], conv_w[l], conv_b[l], w_down[l]),
                       ln2_g[l], ln2_b[l])
    return h
```

```python
import contextlib
import math
import numpy as np
import ml_dtypes
import concourse.bass as bass
import concourse.mybir as mybir
from concourse.bass_utils import run_bass_kernel_spmd

F32 = mybir.dt.float32
BF16 = mybir.dt.bfloat16
I32 = mybir.dt.int32
AF = mybir.ActivationFunctionType
ALU = mybir.AluOpType

D_MODEL = 1024
D_IN = 2840
D_FF = 2816
LN_EPS = 1e-5
RMS_EPS = 1e-5
ALPHA = 2.0 ** 0.25
LAM_INIT = 0.2
ROPE_THETA = 500000.0

ENGS = ["pe", "act", "dve", "pool", "sp"]
_BANK_KEYS = set("b%d" % i for i in range(8))
NDMA = 8


class Prog:
    def __init__(self, nc):
        self.nc = nc
        self.ops = {e: [] for e in ENGS}
        self.known = {e: {} for e in ENGS}
        self.clock = {}
        self.cnt = {}
        self.lastw = {}
        self.readers = {}
        self.marked = set()
        self.dma_rr = {e: 0 for e in ENGS}

    def _deps(self, r, w):
        deps = set()
        for k in r:
            ev = self.lastw.get(k)
            if ev is not None:
                deps.add(ev)
        for k in w:
            ev = self.lastw.get(k)
            if ev is not None:
                deps.add(ev)
            for ev in self.readers.get(k, ()):
                deps.add(ev)
        return deps

    def _resolve(self, eng, deps):
        kn = self.known[eng]
        waits = []
        for (s, i) in sorted(deps, key=lambda t: (t[0], -t[1])):
            if eng == "pe" and s == "pe":
                continue
            if kn.get(s, 0) >= i:
                continue
            waits.append((s, i))
            self.marked.add((s, i))
            for s2, i2 in self.clock[(s, i)].items():
                if kn.get(s2, 0) < i2:
                    kn[s2] = i2
        return waits

    def _record(self, ev, r, w):
        for k in w:
            self.lastw[k] = ev
            self.readers[k] = []
        for k in r:
            if k in w:
                continue
            lst = self.readers.setdefault(k, [])
            for n, old in enumerate(lst):
                if old[0] == ev[0]:
                    lst[n] = ev
                    break
            else:
                lst.append(ev)

    def add(self, eng, fn, r=(), w=()):
        bank_r = [k for k in r if k in _BANK_KEYS and k not in w]
        if bank_r:
            w = list(w) + bank_r
        deps = self._deps(r, w)
        waits = self._resolve(eng, deps)
        idx = self.cnt.get(eng, 0) + 1
        self.cnt[eng] = idx
        ev = (eng, idx)
        ck = dict(self.known[eng])
        ck[eng] = idx
        self.clock[ev] = ck
        self.ops[eng].append(dict(fn=fn, waits=waits, ev=ev, dma=False))
        self._record(ev, r, w)
        return ev

    def dma(self, out, in_, r=(), w=(), q="sp", **kw):
        deps = self._deps(r, w)
        j = self.dma_rr[q]
        self.dma_rr[q] = (j + 1) % NDMA
        stream = "d_%s_%d" % (q, j)
        prev = self.cnt.get(stream, 0)
        if prev > 0:
            deps.add((stream, prev))
        waits = self._resolve(q, deps)
        idx = prev + 1
        self.cnt[stream] = idx
        ev = (stream, idx)
        ck = dict(self.known[q])
        ck[stream] = idx
        self.clock[ev] = ck

        def fn(e, out=out, in_=in_, kw=kw):
            return e.dma_start(out=out, in_=in_, **kw)
        self.ops[q].append(dict(fn=fn, waits=waits, ev=(q, 0), dma=True, dev=ev))
        self._record(ev, r, w)
        return ev

    def barrier(self):
        latest = [(s, i) for s, i in self.cnt.items() if i > 0]
        for e in ENGS:
            waits = self._resolve(e, set(latest))
            if waits:
                self.ops[e].append(dict(fn=None, waits=waits, ev=(e, 0), dma=False))
        self.lastw = {}
        self.readers = {}

    def emit(self):
        nc = self.nc
        streams = sorted(self.cnt.keys())
        rank = {}
        for s in streams:
            c = 0
            if s.startswith("d_"):
                for i in range(1, self.cnt[s] + 1):
                    rank[(s, i)] = 16 * i
            else:
                for i in range(1, self.cnt[s] + 1):
                    if (s, i) in self.marked:
                        c += 1
                        rank[(s, i)] = c
        with contextlib.ExitStack() as st:
            sems = {}
            for s in streams:
                sems[s] = st.enter_context(nc.semaphore("sem_" + s))
            block = st.enter_context(nc.Block())
            ops = self.ops
            marked = self.marked

            def run(e, name):
                for op in ops[name]:
                    for (s, i) in op["waits"]:
                        e.wait_ge(sems[s], rank[(s, i)])
                    if op["fn"] is None:
                        continue
                    ins = op["fn"](e)
                    if op["dma"]:
                        ins.then_inc(sems[op["dev"][0]], 16)
                    elif op["ev"] in marked:
                        ins.then_inc(sems[name], 1)

            @block.tensor
            def _(e):
                run(e, "pe")

            @block.scalar
            def _(e):
                run(e, "act")

            @block.vector
            def _(e):
                run(e, "dve")

            @block.gpsimd
            def _(e):
                run(e, "pool")

            @block.sync
            def _(e):
                run(e, "sp")


class Arena:
    def __init__(self, ap, nbytes):
        self.ap = ap
        self.nbytes = nbytes
        self.off = 0

    def alloc(self, shape, dtype):
        esz = 2 if dtype == BF16 else 4
        n = 1
        for s in shape:
            n *= s
        nb = n * esz
        self.off = (self.off + 63) // 64 * 64
        assert self.off + nb <= self.nbytes, ("arena overflow", self.off, nb, self.nbytes)
        a = self.ap[:, self.off // 2:(self.off + nb) // 2]
        self.off += nb
        if dtype != BF16:
            a = a.bitcast(dtype)
        if len(shape) == 2:
            a = a.rearrange("p (a b) -> p a b", b=shape[1])
        elif len(shape) == 3:
            a = a.rearrange("p (a b c) -> p a b c", b=shape[1], c=shape[2])
        return a

    def mark(self):
        return self.off

    def reset(self, m):
        self.off = m


def host_consts(S):
    NT = S // 128
    c = {}
    c["ident"] = np.eye(128, dtype=np.float32).astype(ml_dtypes.bfloat16)
    k = np.arange(128)[:, None]
    q = np.arange(128)[None, :]
    c["tri_le"] = (k <= q).astype(np.float32).astype(ml_dtypes.bfloat16)
    c["tri_gt"] = (k > q).astype(np.float32).astype(ml_dtypes.bfloat16)
    perm = np.zeros((128, 128), np.float32)
    invf = np.zeros((128, 1), np.float32)
    sgn = np.zeros((128, 1), np.float32)
    inv_freq = (1.0 / (ROPE_THETA ** (np.arange(8, dtype=np.float32) / np.float32(8)))).astype(np.float32)
    for m in range(128):
        mm = m % 64
        if mm < 8:
            perm[m + 8, m] = 1.0
            invf[m, 0] = inv_freq[mm]
            sgn[m, 0] = -1.0
        elif mm < 16:
            perm[m - 8, m] = 1.0
            invf[m, 0] = inv_freq[mm - 8]
            sgn[m, 0] = 1.0
    c["perm"] = perm.astype(ml_dtypes.bfloat16)
    cols = np.zeros((128, 4), np.float32)
    cols[:, 0:1] = invf
    cols[:, 1:2] = sgn * 6.28318
    cols[:, 2] = LN_EPS
    cols[:, 3] = 0.0
    c["cols"] = cols
    c["onehot"] = (np.arange(S)[None, :] // 64 == np.arange(64)[:, None]).astype(np.float32).astype(ml_dtypes.bfloat16)
    n_cmp = S // 16 - 1
    n_slc = S // 64
    NCT = (n_cmp + 127) // 128
    cs = np.arange(NCT * 128)[:, None] * 16
    ss = np.arange(64)[None, :] * 64
    ov = np.clip(np.minimum(cs + 32, ss + 64) - np.maximum(cs, ss), 0, None) / 32.0
    ov[n_cmp:, :] = 0.0
    c["ovl"] = ov.astype(np.float32).astype(ml_dtypes.bfloat16)
    force = np.zeros((128, NT, 64), np.float32)
    for qt in range(NT):
        for p in range(128):
            cur = (qt * 128 + p) // 64
            force[p, qt, 0] = 1e6
            force[p, qt, cur] = 1e6
            if cur >= 1:
                force[p, qt, cur - 1] = 1e6
    c["force"] = force
    c["val16"] = (16.0 * np.arange(128)[:, None] - np.arange(512)[None, :]).astype(np.float32)
    return c


CONST_SPECS = lambda S: [
    ("ident", [128, 128], BF16), ("tri_le", [128, 128], BF16), ("tri_gt", [128, 128], BF16),
    ("perm", [128, 128], BF16), ("cols", [128, 4], F32), ("onehot", [64, S], BF16),
    ("ovl", [((S // 16 - 1 + 127) // 128) * 128, 64], BF16), ("force", [128, S // 128, 64], F32),
    ("val16", [128, 512], F32),
]

PARAM_SPECS = [
    ("w_in", [1024, D_IN]), ("lambda_q1", [1, 64]), ("lambda_k1", [1, 64]), ("lambda_q2", [1, 64]),
    ("lambda_k2", [1, 64]), ("diff_norm_g", [1, 128]),
    ("cmp_pe_k", [32, 64]), ("cmp_w1_k", [2048, 128]), ("cmp_b1_k", [128, 1]), ("cmp_w2_k", [128, 64]),
    ("cmp_pe_v", [32, 64]), ("cmp_w1_v", [2048, 128]), ("cmp_b1_v", [128, 1]), ("cmp_w2_v", [128, 64]),
    ("w_out", [1024, 1024]), ("ln1_g", [1, 1024]), ("ln1_b", [1, 1024]),
    ("w_up", [1024, 2 * D_FF]), ("conv_w", [3, 2 * D_FF]), ("conv_b", [1, 2 * D_FF]),
    ("w_down", [D_FF, 1024]), ("ln2_g", [1, 1024]), ("ln2_b", [1, 1024]),
]


def _ns(d):
    import types
    return types.SimpleNamespace(**d)


def build(S, stop_after=None):
    assert S % 512 == 0
    NT = S // 128
    NQC = S // 512
    n_cmp = S // 16 - 1
    NCT = (n_cmp + 127) // 128
    nc = bass.Bass("TRN2", target_bir_lowering=False)
    dr = {}
    dr["x"] = nc.dram_tensor("x", [S, 1024], F32, kind="ExternalInput").ap()
    dr["pos"] = nc.dram_tensor("pos", [1, S], I32, kind="ExternalInput").ap()
    for name, shp in PARAM_SPECS:
        dr[name] = nc.dram_tensor(name, shp, F32, kind="ExternalInput").ap()
    for name, shp, dt in CONST_SPECS(S):
        dr[name] = nc.dram_tensor("c_" + name, shp, dt, kind="ExternalInput").ap()
    y = nc.dram_tensor("y", [S, 1024], F32, kind="ExternalOutput").ap()
    fm = nc.dram_tensor("s_fm", [16, 128, S], BF16, kind="Internal").ap()
    tm = nc.dram_tensor("s_tm", [S, 768], BF16, kind="Internal").ap()
    gt = nc.dram_tensor("s_gt", [S, 24], F32, kind="Internal").ap()
    mix = nc.dram_tensor("s_mix", [S, 1024], BF16, kind="Internal").ap()
    h1s = nc.dram_tensor("s_h1", [S, 1024], F32, kind="Internal").ap()
    h1T = nc.dram_tensor("s_h1T", [8, 128, S], BF16, kind="Internal").ap()

    ARENA_BYTES = 206 * 1024
    with contextlib.ExitStack() as st:
        arena_t = st.enter_context(nc.sbuf_tensor("arena", [128, ARENA_BYTES // 2], BF16))
        psum_t = st.enter_context(nc.psum_tensor("psum", [128, 4096], F32))
        P = Prog(nc)
        A = Arena(arena_t, ARENA_BYTES)

        def bank(b, c0=0, n=512):
            return psum_t[:, b * 512 + c0:b * 512 + c0 + n]

        def bank_bf(b):
            return psum_t[:, b * 512:(b + 1) * 512].bitcast(BF16)

        def mm(out, lhsT, rhs, start, stop, r, w):
            P.add("pe", lambda e: e.matmul(out, lhsT=lhsT, rhs=rhs, start=start, stop=stop), r=r, w=w)

        def tr(out, in_, ident, r, w):
            P.add("pe", lambda e: e.transpose(out=out, in_=in_, identity=ident), r=r, w=w)

        def act(out, in_, func, r, w, **kw):
            P.add("act", lambda e: e.activation(out=out, in_=in_, func=func, **kw), r=r, w=w)

        def cp(eng, out, in_, r, w):
            if eng == "act":
                P.add("act", lambda e: e.activation(out=out, in_=in_, func=AF.Copy), r=r, w=w)
            else:
                P.add(eng, lambda e: e.tensor_copy(out=out, in_=in_), r=r, w=w)

        def tt(eng, out, in0, in1, op, r, w):
            P.add(eng, lambda e: e.tensor_tensor(out=out, in0=in0, in1=in1, op=op), r=r, w=w)

        def ts(eng, out, in0, s1, s2, op0, op1, r, w):
            if op1 is None:
                P.add(eng, lambda e: e.tensor_scalar(out=out, in0=in0, scalar1=s1, scalar2=None, op0=op0), r=r, w=w)
            else:
                P.add(eng, lambda e: e.tensor_scalar(out=out, in0=in0, scalar1=s1, scalar2=s2, op0=op0, op1=op1), r=r, w=w)

        def stt(out, in0, scalar, in1, op0, op1, r, w):
            P.add("dve", lambda e: e.scalar_tensor_tensor(out=out, in0=in0, scalar=scalar, in1=in1, op0=op0, op1=op1), r=r, w=w)

        def recip(out, in_, r, w):
            P.add("dve", lambda e: e.reciprocal(out=out, in_=in_), r=r, w=w)

        def memset(eng, ap, val, w):
            P.add(eng, lambda e: e.memset(ap, val), r=(), w=w)

        ident = A.alloc([128], BF16)
        tri_le = A.alloc([128], BF16)
        tri_gt = A.alloc([128], BF16)
        perm = A.alloc([128], BF16)
        cols = A.alloc([4], F32)
        invf = cols[:, 0:1]
        sgn2pi = cols[:, 1:2]
        epsc = cols[:, 2:3]
        Ckc = A.alloc([NCT * 128], F32)
        Skc = A.alloc([NCT * 128], F32)
        neglam = A.alloc([1], F32)
        g08 = A.alloc([128], F32)
        for nm, t_ in (("ident", ident), ("tri_le", tri_le), ("tri_gt", tri_gt), ("perm", perm), ("cols", cols)):
            P.dma(t_, dr[nm][:, :], w=["c_" + nm])
        pm = A.mark()
        lq = A.alloc([4, 64], F32)
        for i, nm in enumerate(["lambda_q1", "lambda_k1", "lambda_q2", "lambda_k2"]):
            P.dma(lq[:, i, :], dr[nm][0:1, :].to_broadcast([128, 64]), w=["lq%d" % i])
        lt = A.alloc([2, 64], F32)
        ls = A.alloc([2], F32)
        tt("dve", lt[:, 0, :], lq[:, 0, :], lq[:, 1, :], ALU.mult, r=["lq0", "lq1"], w=["lt0"])
        tt("dve", lt[:, 1, :], lq[:, 2, :], lq[:, 3, :], ALU.mult, r=["lq2", "lq3"], w=["lt1"])
        P.add("dve", lambda e: e.reduce_sum(out=ls[:, 0:1], in_=lt[:, 0, :], axis=mybir.AxisListType.X), r=["lt0"], w=["ls0"])
        P.add("dve", lambda e: e.reduce_sum(out=ls[:, 1:2], in_=lt[:, 1, :], axis=mybir.AxisListType.X), r=["lt1"], w=["ls1"])
        act(ls, ls, AF.Exp, r=["ls0", "ls1"], w=["lse"])
        stt(neglam, ls[:, 1:2], -LAM_INIT, ls[:, 0:1], ALU.add, ALU.subtract, r=["lse"], w=["neglam"])
        P.dma(g08, dr["diff_norm_g"][0:1, :].to_broadcast([128, 128]), w=["g08"])
        ts("dve", g08, g08, 1.0 - LAM_INIT, None, ALU.mult, None, r=["g08"], w=["g08"])
        P.barrier()
        A.reset(pm)
        persist_mark = A.mark()

        w_in_bf = A.alloc([8, D_IN], BF16)
        xT = A.alloc([8, S], BF16)
        Ck = A.alloc([S], F32)
        Sk = A.alloc([S], F32)
        wst = A.alloc([D_IN], F32)
        xtf = [A.alloc([1024], F32) for _ in range(2)]
        xtb = [A.alloc([1024], BF16) for _ in range(2)]
        Abf = [A.alloc([512], BF16) for _ in range(2)]
        t1 = [A.alloc([512], F32) for _ in range(2)]
        t2 = [A.alloc([512], F32) for _ in range(2)]
        ob = [A.alloc([512], BF16) for _ in range(3)]
        vtok = [A.alloc([768], BF16) for _ in range(2)]
        gts = [A.alloc([24], F32) for _ in range(2)]

        xflat = xT.rearrange("p a b -> p (a b)").bitcast(F32)
        tv = xflat[:, 0:S]
        tw = xflat[:, S:2 * S]
        tki = xflat[:, 2 * S:3 * S].bitcast(I32)
        tg = xflat[:, 3 * S:4 * S]
        tpi = xflat[:, 3 * S:4 * S].bitcast(I32)
        P.dma(tpi, dr["pos"][0:1, :].to_broadcast([128, S]), w=["tg"])
        cp("dve", tw, tpi, r=["tg"], w=["tw"])
        ts("dve", tv, tw, invf, 1.0 / (2.0 * math.pi), ALU.mult, ALU.mult, r=["tw"], w=["tv"])

        def frac_to(dst, scale):
            cp("dve", tki, tv, r=["tv"], w=["tki"])
            cp("dve", tw, tki, r=["tki"], w=["tw"])
            tt("dve", tw, tv, tw, ALU.subtract, r=["tv", "tw"], w=["tw"])
            ts("dve", tg, tw, 0.5, None, ALU.is_gt, None, r=["tw"], w=["tg"])
            tt("dve", tw, tw, tg, ALU.subtract, r=["tw", "tg"], w=["tw"])
            ts("dve", tg, tw, -0.5, None, ALU.is_lt, None, r=["tw"], w=["tg"])
            tt("dve", tw, tw, tg, ALU.add, r=["tw", "tg"], w=["tw"])
            act(dst, tw, AF.Sin, r=["tw"], w=["tbl"], scale=scale)

        frac_to(Sk, sgn2pi)
        ts("dve", tv, tv, 0.25, None, ALU.add, None, r=["tv", "tbl"], w=["tv"])
        frac_to(Ck, 6.28318)
        ckv = Ck[:, 16:S].rearrange("p (n s) -> p n s", s=16)
        skv = Sk[:, 16:S].rearrange("p (n s) -> p n s", s=16)
        memset("dve", Ckc, 0.0, w=["Ckc"])
        memset("dve", Skc, 0.0, w=["Skc"])
        cp("dve", Ckc[:, 0:n_cmp], ckv[:, :, 15], r=["tbl", "Ckc"], w=["Ckc"])
        cp("dve", Skc[:, 0:n_cmp], skv[:, :, 15], r=["tbl", "Skc"], w=["Skc"])
        P.barrier()

        for fc in range(8):
            P.dma(wst, dr["w_in"][fc * 128:(fc + 1) * 128, :], w=["wst"])
            cp("dve", w_in_bf[:, fc, 0:1420], wst[:, 0:1420], r=["wst"], w=["winb%d" % fc])
            cp("pool", w_in_bf[:, fc, 1420:D_IN], wst[:, 1420:D_IN], r=["wst"], w=["winc%d" % fc])
        for t_i in range(NT):
            sl = t_i % 2
            P.dma(xtf[sl], dr["x"][t_i * 128:(t_i + 1) * 128, :], w=["xtf%d" % sl])
            cp("pool", xtb[sl], xtf[sl], r=["xtf%d" % sl], w=["xtb%d" % sl])
            pst = bank_bf(7)
            for fc in range(8):
                tr(pst[:, fc * 128:(fc + 1) * 128], xtb[sl][:, fc * 128:(fc + 1) * 128], ident,
                   r=["xtb%d" % sl], w=["b7"])
            cp("act", xT[:, :, t_i * 128:(t_i + 1) * 128], pst.rearrange("p (a b) -> p a b", b=128),
               r=["b7"], w=["xT%d" % t_i])
        WIN_KEYS = ["winb%d" % i for i in range(8)] + ["winc%d" % i for i in range(8)]
        CH_COLS = [0, 128, 256, 384, 512, 640, 768, 896, 1536, 1664, 1792, 1920, 2048, 2176, 2304, 2560]
        CH_ROPE = [True] * 12 + [False, False, True, True]
        it = 0
        for tc in range(NQC):
            xkeys = ["xT%d" % (tc * 4 + i) for i in range(4)]
            for ch in range(16):
                b = 4 + (it % 2)
                s2 = it % 2
                s3 = it % 3
                c0 = CH_COLS[ch]
                for fc in range(8):
                    mm(bank(b), w_in_bf[:, fc, c0:c0 + 128], xT[:, fc, tc * 512:(tc + 1) * 512],
                       fc == 0, fc == 7, r=xkeys + WIN_KEYS, w=["b%d" % b])
                if CH_ROPE[ch]:
                    cp("act", Abf[s2], bank(b), r=["b%d" % b], w=["Abf%d" % s2])
                    mm(bank(6), perm, Abf[s2], True, True, r=["Abf%d" % s2], w=["b6"])
                    tt("dve", t1[s2], bank(b), Ck[:, tc * 512:(tc + 1) * 512], ALU.mult, r=["b%d" % b], w=["t1%d" % s2])
                    tt("dve", t2[s2], bank(6), Sk[:, tc * 512:(tc + 1) * 512], ALU.mult, r=["b6"], w=["t2%d" % s2])
                    tt("pool", ob[s3], t1[s2], t2[s2], ALU.add, r=["t1%d" % s2, "t2%d" % s2], w=["ob%d" % s3])
                else:
                    cp("act", ob[s3], bank(b), r=["b%d" % b], w=["ob%d" % s3])
                P.dma(fm[ch, :, tc * 512:(tc + 1) * 512], ob[s3], r=["ob%d" % s3], w=["fm%d" % ch])
                it += 1
        for t_i in range(NT):
            sl = t_i % 2
            ba = 0 + sl
            bb = 2 + sl
            xk = ["xT%d" % t_i]
            for fc in range(8):
                mm(bank(ba), xT[:, fc, t_i * 128:(t_i + 1) * 128], w_in_bf[:, fc, 1024:1536], fc == 0, fc == 7,
                   r=xk + WIN_KEYS, w=["b%d" % ba])
            for fc in range(8):
                mm(bank(bb, 0, 128), xT[:, fc, t_i * 128:(t_i + 1) * 128], w_in_bf[:, fc, 2432:2560], fc == 0, fc == 7,
                   r=xk + WIN_KEYS, w=["b%d" % bb])
            for fc in range(8):
                mm(bank(bb, 128, 152), xT[:, fc, t_i * 128:(t_i + 1) * 128], w_in_bf[:, fc, 2688:2840], fc == 0, fc == 7,
                   r=xk + WIN_KEYS, w=["b%d" % bb])
            cp("act", vtok[sl][:, 0:512], bank(ba), r=["b%d" % ba], w=["vtokA%d" % sl])
            cp("dve", vtok[sl][:, 512:768], bank(bb, 0, 256), r=["b%d" % bb], w=["vtokB%d" % sl])
            act(gts[sl], bank(bb, 256, 24), AF.Sigmoid, r=["b%d" % bb], w=["gts%d" % sl])
            P.dma(tm[t_i * 128:(t_i + 1) * 128, :], vtok[sl], r=["vtokA%d" % sl, "vtokB%d" % sl], w=["tm"])
            P.dma(gt[t_i * 128:(t_i + 1) * 128, :], gts[sl], r=["gts%d" % sl], w=["gt"])
        P.barrier()
        A.reset(persist_mark)

        ET = [A.alloc([512], BF16) for _ in range(3)]
        att_mark = A.mark()
        st_cnt = [0]

        def attn_tile(lhsT, rhs_fn, kdeps, sub0, sub1, mask, mask_sub, v1, v1deps, acc_w, first_fn, last_fn, accn):
            i = st_cnt[0]
            st_cnt[0] += 1
            sb = 4 + (i % 3)
            es = i % 3
            ncols = (sub1 - sub0) * 128
            mm(bank(sb, 0, ncols), lhsT, rhs_fn(sub0, sub1), True, True, r=kdeps, w=["b%d" % sb])
            act(ET[es][:, 0:ncols], bank(sb, 0, ncols), AF.Exp, r=["b%d" % sb], w=["ET%d" % es], scale=0.125)
            if mask is not None:
                if callable(mask):
                    mask(ET[es], ncols, "ET%d" % es)
                else:
                    c_ = (mask_sub - sub0) * 128
                    tt("pool", ET[es][:, c_:c_ + 128], ET[es][:, c_:c_ + 128], mask, ALU.mult,
                       r=["ET%d" % es], w=["ET%d" % es])
            for sub in range(sub0, sub1):
                c_ = (sub - sub0) * 128
                mm(bank(sub, 0, accn), ET[es][:, c_:c_ + 128], v1, first_fn(sub), last_fn(sub),
                   r=["ET%d" % es] + v1deps, w=["b%d" % sub])

        if stop_after != "p1":
            KT = [A.alloc([S], BF16) for _ in range(2)]
            QT = [A.alloc([S], BF16) for _ in range(2)]
            V1 = [A.alloc([NT, 129], BF16) for _ in range(2)]
            O1 = A.alloc([4, 129], F32)
            r1 = A.alloc([4], F32)
            r2 = A.alloc([4], F32)
            tmpo = A.alloc([128], F32)
            od = A.alloc([128], F32)
            junk = A.alloc([128], F32)
            ssq = A.alloc([4], F32)
            odn = [A.alloc([4, 128], BF16) for _ in range(2)]
            for kb in range(2):
                memset("pool", V1[kb][:, :, 128:129], 1.0, w=["V1o%d" % kb])

            def load_head(h):
                kb = h % 2
                P.dma(KT[kb], fm[4 + h, :, :], w=["KT%d" % kb])
                P.dma(QT[kb], fm[h, :, :], w=["QT%d" % kb])
                vsrc = tm[:, h * 128:(h + 1) * 128].rearrange("(n p) c -> p n c", p=128)
                for n0 in range(0, NT, 8):
                    n1 = min(NT, n0 + 8)
                    P.dma(V1[kb][:, n0:n1, 0:128], vsrc[:, n0:n1, :], w=["V1d%d_%d" % (kb, n0)])

            load_head(0)
            fin = 0
            for h in range(4):
                kb = h % 2
                if h + 1 < 4:
                    load_head(h + 1)
                v1deps = ["V1o%d" % kb] + ["V1d%d_%d" % (kb, n0) for n0 in range(0, NT, 8)]
                for qc in range(NQC):
                    for c in range(2):
                        rows = slice(64 * c, 64 * c + 64)
                        nkt = 4 * qc + 4
                        for kt in range(nkt):
                            j = kt - 4 * qc
                            sub0 = max(j, 0)
                            i = st_cnt[0]
                            st_cnt[0] += 1
                            sb = 4 + (i % 3)
                            es = i % 3
                            ncols = (4 - sub0) * 128
                            mm(bank(sb, 0, ncols), KT[kb][rows, kt * 128:(kt + 1) * 128],
                               QT[kb][rows, qc * 512 + sub0 * 128:(qc + 1) * 512], True, True,
                               r=["KT%d" % kb, "QT%d" % kb], w=["b%d" % sb])
                            act(ET[es][:, 0:ncols], bank(sb, 0, ncols), AF.Exp, r=["b%d" % sb], w=["ET%d" % es], scale=0.125)
                            if j >= 0:
                                tt("pool", ET[es][:, 0:128], ET[es][:, 0:128], tri_le, ALU.mult,
                                   r=["ET%d" % es], w=["ET%d" % es])
                            for sub in range(sub0, 4):
                                c_ = (sub - sub0) * 128
                                mm(bank(sub, 0, 129), ET[es][:, c_:c_ + 128], V1[kb][:, kt, :],
                                   kt == 0, kt == 4 * qc + sub, r=["ET%d" % es] + v1deps, w=["b%d" % sub])
                        if c == 0:
                            for sub in range(4):
                                cp("act", O1[:, sub, :], bank(sub, 0, 129), r=["b%d" % sub], w=["O1_%d" % sub])
                    fs = fin % 2
                    fin += 1
                    recip(r1, O1[:, :, 128], r=["O1_%d" % s_ for s_ in range(4)], w=["r1"])
                    for sub in range(4):
                        recip(r2[:, sub:sub + 1], bank(sub, 128, 1), r=["b%d" % sub], w=["r2"])
                        tt("dve", r2[:, sub:sub + 1], r2[:, sub:sub + 1], neglam, ALU.mult, r=["r2"], w=["r2"])
                        ts("dve", tmpo, O1[:, sub, 0:128], r1[:, sub:sub + 1], None, ALU.mult, None,
                           r=["O1_%d" % sub, "r1"], w=["tmpo"])
                        stt(od, bank(sub, 0, 128), r2[:, sub:sub + 1], tmpo, ALU.mult, ALU.add,
                            r=["b%d" % sub, "r2", "tmpo"], w=["od"])
                        act(junk, od, AF.Square, r=["od"], w=["junk", "ssq"], accum_out=ssq[:, sub:sub + 1])
                        act(ssq[:, sub:sub + 1], ssq[:, sub:sub + 1], AF.Sqrt, r=["ssq"], w=["ssq"],
                            scale=1.0 / 128.0, bias=epsc)
                        recip(ssq[:, sub:sub + 1], ssq[:, sub:sub + 1], r=["ssq"], w=["ssq"])
                        stt(odn[fs][:, sub, :], od, ssq[:, sub:sub + 1], g08, ALU.mult, ALU.mult,
                            r=["od", "ssq"], w=["odn%d" % fs])
                    P.dma(mix[qc * 512:(qc + 1) * 512, h * 128:(h + 1) * 128].rearrange("(s p) c -> p s c", p=128),
                          odn[fs], r=["odn%d" % fs], w=["mix"])
            P.barrier()
        A.reset(att_mark)

        if stop_after not in ("p1", "p2"):
            build_nsa(_ns(locals()))
        P.barrier()
        A.reset(persist_mark)
        if stop_after not in ("p1", "p2", "p3"):
            build_tail(_ns(locals()))
        P.barrier()
        P.emit()
    return nc


def core_inputs(inputs, b, S):
    c = host_consts(S)
    m = {"x": np.ascontiguousarray(inputs["x"][b], dtype=np.float32),
         "pos": np.ascontiguousarray(inputs["positions"][b:b + 1], dtype=np.int32)}
    for name, shp in PARAM_SPECS:
        m[name] = np.ascontiguousarray(np.asarray(inputs[name][0], dtype=np.float32).reshape(shp))
    for name, shp, dt in CONST_SPECS(S):
        m["c_" + name] = np.ascontiguousarray(c[name])
    return m


def build_nsa(ns):
    P, A, S, NT, NQC, n_cmp, NCT = ns.P, ns.A, ns.S, ns.NT, ns.NQC, ns.n_cmp, ns.NCT
    bank, bank_bf, mm, tr, act, cp, tt, ts, stt, recip, memset = (ns.bank, ns.bank_bf, ns.mm, ns.tr, ns.act, ns.cp,
                                                                   ns.tt, ns.ts, ns.stt, ns.recip, ns.memset)
    fm, tm, gt, mix, dr = ns.fm, ns.tm, ns.gt, ns.mix, ns.dr
    ET, st_cnt = ns.ET, ns.st_cnt
    ident, tri_le, tri_gt, perm, Ckc, Skc = ns.ident, ns.tri_le, ns.tri_gt, ns.perm, ns.Ckc, ns.Skc
    NCP = NCT * 128

    KTs = A.alloc([2, S], BF16)
    KTw = A.alloc([2, S], BF16)
    QTa = A.alloc([2, 8, 512], BF16)
    V1s = A.alloc([2, NT, 65], BF16)
    V1w = A.alloc([2, NT, 65], BF16)
    kcmpT = A.alloc([2, NCP], BF16)
    V1c = A.alloc([2, NCT, 129], BF16)
    force = A.alloc([NT, 64], F32)
    val16 = A.alloc([512], F32)
    gtile = A.alloc([2, 4, 24], F32)
    on_acc = A.alloc([4, 512], F32)
    on_bf = A.alloc([2, 4, 512], BF16)
    imp = A.alloc([4, 2, 64], F32)
    imp2 = A.alloc([64], F32)
    imp3 = A.alloc([64], F32)
    m8 = A.alloc([8], F32)
    m8b = A.alloc([8], F32)
    negpad = A.alloc([4, 128], BF16)
    rc = A.alloc([4], F32)
    gc = A.alloc([4], F32)

    P.dma(force, dr["force"][:, :, :], w=["c_force"])
    P.dma(val16, dr["val16"][:, :], w=["c_val16"])
    memset("pool", negpad, 0.0, w=["negpad"])
    for g in range(2):
        P.dma(KTs[0:64, g, :], fm[14, 64 * g:64 * g + 64, :], w=["KTs"])
        P.dma(KTs[64:128, g, :], dr["onehot"][:, :], w=["KTs"])
        P.dma(KTw[0:64, g, :], fm[15, 64 * g:64 * g + 64, :], w=["KTw"])
        memset("pool", V1s[:, g, :, 64:65], 1.0, w=["V1s"])
        memset("pool", V1w[:, g, :, 64:65], 1.0, w=["V1w"])
        vs_src = tm[:, 512 + 64 * g:512 + 64 * g + 64].rearrange("(n p) c -> p n c", p=128)
        vw_src = tm[:, 640 + 64 * g:640 + 64 * g + 64].rearrange("(n p) c -> p n c", p=128)
        for n0 in range(0, NT, 8):
            n1 = min(NT, n0 + 8)
            P.dma(V1s[:, g, n0:n1, 0:64], vs_src[:, n0:n1, :], w=["V1s"])
            P.dma(V1w[:, g, n0:n1, 0:64], vw_src[:, n0:n1, :], w=["V1w"])
        memset("pool", V1c[:, g, :, 64:65], 1.0, w=["V1c"])
        P.dma(V1c[:, g, :, 65:129], dr["ovl"].rearrange("(t p) j -> p t j", p=128), w=["V1c"])

    cm = A.mark()
    srcT = A.alloc([2, 2, S], BF16)
    w1st = A.alloc([32, 128], F32)
    w1bf = A.alloc([2, 32, 128], BF16)
    pest = A.alloc([64], F32)
    pebf = A.alloc([64], BF16)
    peT = A.alloc([2, 32], BF16)
    w2st = A.alloc([64], F32)
    w2bf = A.alloc([2, 64], BF16)
    b1c = A.alloc([2], F32)
    b1e = A.alloc([2], F32)
    z = A.alloc([NCP], F32)
    z2 = A.alloc([NCP], F32)
    zu = A.alloc([NCP], F32)
    hidb = A.alloc([NCP], BF16)
    Ab = A.alloc([NCP], BF16)
    ct1 = A.alloc([NCP], F32)
    ct2 = A.alloc([NCP], F32)
    memset("pool", hidb, 0.0, w=["hidb"])
    for wi, (nw1, npe, nb1, nw2, chn) in enumerate([("cmp_w1_k", "cmp_pe_k", "cmp_b1_k", "cmp_w2_k", 12),
                                                    ("cmp_w1_v", "cmp_pe_v", "cmp_b1_v", "cmp_w2_v", 13)]):
        for g in range(2):
            P.dma(srcT[0:64, wi, g, :], fm[chn, 64 * g:64 * g + 64, :], w=["srcT%d" % wi])
        P.dma(w1st[0:64], dr[nw1].rearrange("(l d) h -> d l h", d=64), w=["w1st"])
        cp("pool", w1bf[0:64, wi], w1st[0:64], r=["w1st"], w=["w1bf%d" % wi])
        P.dma(pest[0:32, :], dr[npe][:, :], w=["pest"])
        cp("dve", pebf[0:32, :], pest[0:32, :], r=["pest"], w=["pebf"])
        tr(bank_bf(7)[0:64, 0:32], pebf[0:32, 0:64], ident[0:32, 0:32], r=["pebf"], w=["b7"])
        cp("dve", peT[0:64, wi, :], bank_bf(7)[0:64, 0:32], r=["b7"], w=["peT%d" % wi])
        P.dma(w2st, dr[nw2][:, :], w=["w2st"])
        cp("dve", w2bf[:, wi, :], w2st, r=["w2st"], w=["w2bf%d" % wi])
        P.dma(b1c[:, wi:wi + 1], dr[nb1][:, :], w=["b1c%d" % wi])
        for l in range(32):
            mm(bank(5, 0, 1), w1bf[0:64, wi, l, :], peT[0:64, wi, l:l + 1], l == 0, l == 31,
               r=["w1bf%d" % wi, "peT%d" % wi], w=["b5"])
        tt("dve", b1e[:, wi:wi + 1], bank(5, 0, 1), b1c[:, wi:wi + 1], ALU.add, r=["b5", "b1c%d" % wi], w=["b1e%d" % wi])
        for g in range(2):
            v16 = srcT[0:64, wi, g, :].rearrange("p (n s) -> p n s", s=16)
            for l in range(32):
                mm(bank(4, 0, n_cmp), w1bf[0:64, wi, l, :], v16[:, l // 16:l // 16 + n_cmp, l % 16], l == 0, l == 31,
                   r=["w1bf%d" % wi, "srcT%d" % wi], w=["b4"])
            act(z[:, 0:n_cmp], bank(4, 0, n_cmp), AF.Identity, r=["b4", "b1e%d" % wi], w=["z"], bias=b1e[:, wi:wi + 1])
            tt("dve", z2[:, 0:n_cmp], z[:, 0:n_cmp], z[:, 0:n_cmp], ALU.mult, r=["z"], w=["z2"])
            ts("dve", z2[:, 0:n_cmp], z2[:, 0:n_cmp], 0.044715, 1.0, ALU.mult, ALU.add, r=["z2"], w=["z2"])
            tt("dve", zu[:, 0:n_cmp], z2[:, 0:n_cmp], z[:, 0:n_cmp], ALU.mult, r=["z2", "z"], w=["zu"])
            act(zu[:, 0:n_cmp], zu[:, 0:n_cmp], AF.Tanh, r=["zu"], w=["zu"], scale=0.7978845608028654)
            ts("dve", zu[:, 0:n_cmp], zu[:, 0:n_cmp], 1.0, 0.5, ALU.add, ALU.mult, r=["zu"], w=["zu"])
            tt("dve", hidb[:, 0:n_cmp], zu[:, 0:n_cmp], z[:, 0:n_cmp], ALU.mult, r=["zu", "z", "hidb"], w=["hidb"])
            if wi == 0:
                mm(bank(5, 0, NCP)[0:64, :], w2bf[:, wi, :], hidb, True, True, r=["hidb", "w2bf0"], w=["b5"])
                cp("act", Ab[0:64, :], bank(5, 0, NCP)[0:64, :], r=["b5"], w=["Ab"])
                mm(bank(6, 0, NCP)[0:64, :], perm[0:64, 0:64], Ab[0:64, :], True, True, r=["Ab"], w=["b6"])
                tt("dve", ct1[0:64, :], bank(5, 0, NCP)[0:64, :], Ckc[0:64, :], ALU.mult, r=["b5"], w=["ct1"])
                tt("dve", ct2[0:64, :], bank(6, 0, NCP)[0:64, :], Skc[0:64, :], ALU.mult, r=["b6"], w=["ct2"])
                tt("pool", kcmpT[0:64, g, :], ct1[0:64, :], ct2[0:64, :], ALU.add, r=["ct1", "ct2"], w=["kcmpT"])
            else:
                for ntl in range(NCT):
                    mm(bank(5, 0, 64), hidb[:, ntl * 128:(ntl + 1) * 128], w2bf[:, wi, :], True, True,
                       r=["hidb", "w2bf1"], w=["b5"])
                    cp("act", V1c[:, g, ntl, 0:64], bank(5, 0, 64), r=["b5", "V1c"], w=["V1c"])
    A.reset(cm)

    def load_q(qc):
        qs = qc % 2
        for h in range(8):
            P.dma(QTa[0:64, qs, h, :], fm[8 + h // 2, 64 * (h % 2):64 * (h % 2) + 64, qc * 512:(qc + 1) * 512],
                  w=["Qd%d_%d" % (qs, h)])
        P.dma(gtile[:, qs], gt[qc * 512:(qc + 1) * 512, :].rearrange("(s p) c -> p s c", p=128), w=["gat%d" % qs])

    def run_tiles(tiles, lhs_fn, rhs_fn, kdeps, v_fn, vdeps, accn):
        first = {}
        last = {}
        for ti, tl in enumerate(tiles):
            for sub in range(tl[1], tl[2]):
                first.setdefault(sub, ti)
                last[sub] = ti
        for ti, (kt, sub0, sub1, mask, msub, maskfn) in enumerate(tiles):
            i = st_cnt[0]
            st_cnt[0] += 1
            sb = 4 + (i % 3)
            es = i % 3
            ncols = (sub1 - sub0) * 128
            mm(bank(sb, 0, ncols), lhs_fn(kt), rhs_fn(sub0, sub1), True, True, r=kdeps, w=["b%d" % sb])
            act(ET[es][:, 0:ncols], bank(sb, 0, ncols), AF.Exp, r=["b%d" % sb], w=["ET%d" % es], scale=0.125)
            if mask is not None:
                c_ = (msub - sub0) * 128
                tt("pool", ET[es][:, c_:c_ + 128], ET[es][:, c_:c_ + 128], mask, ALU.mult,
                   r=["ET%d" % es], w=["ET%d" % es])
            if maskfn is not None:
                maskfn(ET[es], "ET%d" % es)
            for sub in range(sub0, sub1):
                c_ = (sub - sub0) * 128
                mm(bank(sub, 0, accn), ET[es][:, c_:c_ + 128], v_fn(kt), first[sub] == ti, last[sub] == ti,
                   r=["ET%d" % es] + vdeps, w=["b%d" % sub])

    load_q(0)
    for qc in range(NQC):
        qs = qc % 2
        if qc + 1 < NQC:
            load_q(qc + 1)
        for h in range(8):
            g = h // 4
            r_ = h % 4
            tiles = []
            for ntl in range(NCT):
                if 2048 * ntl + 31 > 512 * qc + 511:
                    continue
                full = (16 * (128 * ntl + 127) + 31 <= 512 * qc) and (128 * ntl + 127 < n_cmp)
                if full:
                    tiles.append((ntl, 0, 4, None, 0, None))
                else:
                    thr = float(512 * qc - 2048 * ntl - 31)

                    def mf(et, key, thr=thr):
                        stt(et, val16, thr, et, ALU.is_le, ALU.mult, r=[key, "c_val16"], w=[key])
                    tiles.append((ntl, 0, 4, None, 0, mf))
            run_tiles(tiles,
                      lambda kt, g=g: kcmpT[0:64, g, kt * 128:(kt + 1) * 128],
                      lambda a, b_, qs=qs, h=h: QTa[0:64, qs, h, a * 128:b_ * 128],
                      ["kcmpT", "Qd%d_%d" % (qs, h)],
                      lambda kt, g=g: V1c[:, g, kt, :], ["V1c"], 129)
            hc = slice(h * 64, (h + 1) * 64)
            for sub in range(4):
                ts("dve", rc[:, sub:sub + 1], bank(sub, 64, 1), 1e-30, None, ALU.max, None, r=["b%d" % sub], w=["rc"])
                recip(rc[:, sub:sub + 1], rc[:, sub:sub + 1], r=["rc"], w=["rc"])
                tt("dve", gc[:, sub:sub + 1], rc[:, sub:sub + 1], gtile[:, qs, sub, 3 * h:3 * h + 1], ALU.mult,
                   r=["rc", "gat%d" % qs], w=["gc"])
                ts("dve", on_acc[:, sub, hc], bank(sub, 0, 64), gc[:, sub:sub + 1], None, ALU.mult, None,
                   r=["b%d" % sub, "gc"], w=["on_acc%d" % h])
                if r_ == 0:
                    ts("dve", imp[:, sub, g, :], bank(sub, 65, 64), rc[:, sub:sub + 1], None, ALU.mult, None,
                       r=["b%d" % sub, "rc"], w=["imp%d" % g])
                else:
                    stt(imp[:, sub, g, :], bank(sub, 65, 64), rc[:, sub:sub + 1], imp[:, sub, g, :], ALU.mult, ALU.add,
                        r=["b%d" % sub, "rc", "imp%d" % g], w=["imp%d" % g])
        for g in range(2):
            for sub in range(4):
                qt = 4 * qc + sub
                tt("dve", imp2, imp[:, sub, g, :], force[:, qt, :], ALU.add, r=["imp%d" % g, "c_force"], w=["imp2"])
                P.add("dve", lambda e: e.max(out=m8, in_=imp2), r=["imp2"], w=["m8"])
                P.add("dve", lambda e: e.match_replace(out=imp3, in_to_replace=m8, in_values=imp2, imm_value=-1e9),
                      r=["imp2", "m8"], w=["imp3"])
                P.add("dve", lambda e: e.max(out=m8b, in_=imp3), r=["imp3"], w=["m8b"])
                ts("dve", negpad[:, sub, 64:128], imp2, m8b[:, 7:8], -30000.0, ALU.is_lt, ALU.mult,
                   r=["imp2", "m8b", "negpad"], w=["negpad"])
                mm(bank(7, sub * 128, 128), negpad[:, sub, :], ident, True, True, r=["negpad"], w=["b7"])
            for r_ in range(4):
                h = 4 * g + r_
                cp("act" if r_ % 2 == 0 else "dve", QTa[64:128, qs, h, :], bank(7)[64:128, :], r=["b7"],
                   w=["Qm%d_%d" % (qs, h)])
        for h in range(8):
            g = h // 4
            tiles = []
            for j in range(4):
                kt = 4 * qc - 4 + j
                if kt >= 0:
                    tiles.append((kt, 0, j + 1, tri_gt, j, None))
            for j in range(4):
                tiles.append((4 * qc + j, j, 4, tri_le, j, None))
            run_tiles(tiles,
                      lambda kt, g=g: KTw[0:64, g, kt * 128:(kt + 1) * 128],
                      lambda a, b_, qs=qs, h=h: QTa[0:64, qs, h, a * 128:b_ * 128],
                      ["KTw", "Qd%d_%d" % (qs, h)],
                      lambda kt, g=g: V1w[:, g, kt, :], ["V1w"], 65)
            hc = slice(h * 64, (h + 1) * 64)
            for sub in range(4):
                recip(rc[:, sub:sub + 1], bank(sub, 64, 1), r=["b%d" % sub], w=["rc"])
                tt("dve", gc[:, sub:sub + 1], rc[:, sub:sub + 1], gtile[:, qs, sub, 3 * h + 2:3 * h + 3], ALU.mult,
                   r=["rc", "gat%d" % qs], w=["gc"])
                stt(on_acc[:, sub, hc], bank(sub, 0, 64), gc[:, sub:sub + 1], on_acc[:, sub, hc], ALU.mult, ALU.add,
                    r=["b%d" % sub, "gc", "on_acc%d" % h], w=["on_acc%d" % h])
        for h in range(8):
            g = h // 4
            tiles = []
            for kt in range(4 * qc + 4):
                j = kt - 4 * qc
                if j >= 0:
                    tiles.append((kt, j, 4, tri_le, j, None))
                else:
                    tiles.append((kt, 0, 4, None, 0, None))
            run_tiles(tiles,
                      lambda kt, g=g: KTs[:, g, kt * 128:(kt + 1) * 128],
                      lambda a, b_, qs=qs, h=h: QTa[:, qs, h, a * 128:b_ * 128],
                      ["KTs", "Qd%d_%d" % (qs, h), "Qm%d_%d" % (qs, h)],
                      lambda kt, g=g: V1s[:, g, kt, :], ["V1s"], 65)
            hc = slice(h * 64, (h + 1) * 64)
            for sub in range(4):
                recip(rc[:, sub:sub + 1], bank(sub, 64, 1), r=["b%d" % sub], w=["rc"])
                tt("dve", gc[:, sub:sub + 1], rc[:, sub:sub + 1], gtile[:, qs, sub, 3 * h + 1:3 * h + 2], ALU.mult,
                   r=["rc", "gat%d" % qs], w=["gc"])
                stt(on_bf[:, qs, sub, hc], bank(sub, 0, 64), gc[:, sub:sub + 1], on_acc[:, sub, hc], ALU.mult, ALU.add,
                    r=["b%d" % sub, "gc", "on_acc%d" % h], w=["on_bf%d" % qs])
        P.dma(mix[qc * 512:(qc + 1) * 512, 512:1024].rearrange("(s p) c -> p s c", p=128), on_bf[:, qs],
              r=["on_bf%d" % qs], w=["mix"])


def build_tail(ns):
    P, A, S, NT = ns.P, ns.A, ns.S, ns.NT
    bank, bank_bf, mm, tr, act, cp, tt, ts, stt, recip, memset = (ns.bank, ns.bank_bf, ns.mm, ns.tr, ns.act, ns.cp,
                                                                   ns.tt, ns.ts, ns.stt, ns.recip, ns.memset)
    mix, dr, h1s, h1T, y = ns.mix, ns.dr, ns.h1s, ns.h1T, ns.y
    ident, epsc = ns.ident, ns.epsc
    base = A.mark()

    def layer_norm(src, dst, gam, bet, tag, gk, bk):
        stats, mv, rstd = ln_tmp
        for half in range(2):
            P.add("dve", lambda e, half=half: e.bn_stats(out=stats[:, half, :], in_=src[:, half * 512:(half + 1) * 512]),
                  r=[tag + "_src"], w=["ln_stats"])
        P.add("dve", lambda e: e.bn_aggr(out=mv, in_=stats.rearrange("p a b -> p (a b)")), r=["ln_stats"], w=["ln_mv"])
        act(rstd, mv[:, 1:2], AF.Sqrt, r=["ln_mv"], w=["ln_rstd"], bias=epsc, scale=1.0)
        recip(rstd, rstd, r=["ln_rstd"], w=["ln_rstd"])
        ts("dve", src, src, mv[:, 0:1], rstd, ALU.subtract, ALU.mult, r=[tag + "_src", "ln_mv", "ln_rstd"], w=[tag + "_src"])
        tt("pool", dst, src, gam, ALU.mult, r=[tag + "_src", gk], w=[tag + "_dst"])
        tt("pool", dst, dst, bet, ALU.add, r=[tag + "_dst", bk], w=[tag + "_dst"])

    w_out_bf = A.alloc([8, 1024], BF16)
    wst = A.alloc([2816], F32)
    g1 = A.alloc([1024], F32)
    b1 = A.alloc([1024], F32)
    mixt = [A.alloc([1024], BF16) for _ in range(2)]
    xt = [A.alloc([1024], F32) for _ in range(2)]
    mixT = A.alloc([8, 128], BF16)
    hpre = A.alloc([1024], F32)
    h1 = [A.alloc([1024], F32) for _ in range(2)]
    h1b = A.alloc([1024], BF16)
    h1Tt = [A.alloc([8, 128], BF16) for _ in range(2)]
    ln_tmp = (A.alloc([2, 6], F32), A.alloc([2], F32), A.alloc([1], F32))
    for fc in range(8):
        P.dma(wst[:, 0:1024], dr["w_out"][fc * 128:(fc + 1) * 128, :], w=["wst"])
        cp("dve" if fc % 2 == 0 else "pool", w_out_bf[:, fc, :], wst[:, 0:1024], r=["wst"], w=["woutb"])
    P.dma(g1, dr["ln1_g"][0:1, :].to_broadcast([128, 1024]), w=["g1"])
    P.dma(b1, dr["ln1_b"][0:1, :].to_broadcast([128, 1024]), w=["b1"])
    h1T_v = h1T.rearrange("f p t -> p f t")
    for t_i in range(NT):
        sl = t_i % 2
        P.dma(mixt[sl], mix[t_i * 128:(t_i + 1) * 128, :], w=["mixt%d" % sl])
        P.dma(xt[sl], dr["x"][t_i * 128:(t_i + 1) * 128, :], w=["xt%d" % sl])
        for fc in range(8):
            tr(bank_bf(7)[:, fc * 128:(fc + 1) * 128], mixt[sl][:, fc * 128:(fc + 1) * 128], ident,
               r=["mixt%d" % sl], w=["b7"])
        cp("act", mixT, bank_bf(7).rearrange("p (a b) -> p a b", b=128), r=["b7"], w=["mixT"])
        for half in range(2):
            for fc in range(8):
                mm(bank(4 + half), mixT[:, fc, :], w_out_bf[:, fc, half * 512:(half + 1) * 512], fc == 0, fc == 7,
                   r=["mixT", "woutb"], w=["b%d" % (4 + half)])
            stt(hpre[:, half * 512:(half + 1) * 512], xt[sl][:, half * 512:(half + 1) * 512], ALPHA, bank(4 + half),
                ALU.mult, ALU.add, r=["xt%d" % sl, "b%d" % (4 + half)], w=["l1_src"])
        P.ln_args = None
        layer_norm(hpre, h1[sl], g1, b1, "l1", "g1", "b1")
        P.dma(h1s[t_i * 128:(t_i + 1) * 128, :], h1[sl], r=["l1_dst"], w=["h1s"])
        cp("act", h1b, h1[sl], r=["l1_dst"], w=["h1b"])
        for fc in range(8):
            tr(bank_bf(6)[:, fc * 128:(fc + 1) * 128], h1b[:, fc * 128:(fc + 1) * 128], ident, r=["h1b"], w=["b6"])
        cp("dve", h1Tt[sl], bank_bf(6).rearrange("p (a b) -> p a b", b=128), r=["b6"], w=["h1Tt%d" % sl])
        P.dma(h1T_v[:, :, t_i * 128:(t_i + 1) * 128], h1Tt[sl], r=["h1Tt%d" % sl], w=["h1T"])
    P.barrier()
    A.reset(base)

    NC5 = S // 256
    w_up_bf = A.alloc([8, 2 * D_FF], BF16)
    w_dn_bf = A.alloc([22, 1024], BF16)
    wst = A.alloc([2816], F32)
    g2 = A.alloc([1024], F32)
    b2 = A.alloc([1024], F32)
    identf = A.alloc([128], F32)
    cwst = A.alloc([4, 128], F32)
    cwT = A.alloc([4, 44], F32)
    h1Tc = [A.alloc([8, 258], BF16) for _ in range(2)]
    Aa = [A.alloc([256], F32) for _ in range(2)]
    bb = [A.alloc([256], F32) for _ in range(2)]
    cc = [A.alloc([256], F32) for _ in range(2)]
    sg = A.alloc([256], F32)
    G = [A.alloc([256], BF16) for _ in range(2)]
    h1r = [A.alloc([1024], F32) for _ in range(2)]
    hpre2 = A.alloc([1024], F32)
    outt = [A.alloc([1024], F32) for _ in range(2)]
    ln_tmp = (A.alloc([2, 6], F32), A.alloc([2], F32), A.alloc([1], F32))
    engs3 = ["dve", "pool", "act"]
    k = 0
    for fc in range(8):
        for half in range(2):
            P.dma(wst, dr["w_up"][fc * 128:(fc + 1) * 128, half * 2816:(half + 1) * 2816], w=["wst"])
            for j3 in range(3):
                c0 = half * 2816 + j3 * 940
                c1 = half * 2816 + min(2816, (j3 + 1) * 940) if j3 < 2 else (half + 1) * 2816
                cp(engs3[j3], w_up_bf[:, fc, c0:c1], wst[:, c0 - half * 2816:c1 - half * 2816], r=["wst"], w=["wupb%d" % j3])
    for i2 in range(11):
        P.dma(wst[:, 0:2048].rearrange("p (a b) -> p a b", b=1024),
              dr["w_down"][i2 * 256:(i2 + 1) * 256, :].rearrange("(a p) c -> p a c", p=128), w=["wst"])
        cp("dve", w_dn_bf[:, 2 * i2, :], wst[:, 0:1024], r=["wst"], w=["wdnb0"])
        cp("pool", w_dn_bf[:, 2 * i2 + 1, :], wst[:, 1024:2048], r=["wst"], w=["wdnb1"])
    P.dma(g2, dr["ln2_g"][0:1, :].to_broadcast([128, 1024]), w=["g2"])
    P.dma(b2, dr["ln2_b"][0:1, :].to_broadcast([128, 1024]), w=["b2"])
    cp("dve", identf, ident, r=[], w=["identf"])
    for kk in range(3):
        P.dma(cwst[0:44, kk, :], dr["conv_w"][kk:kk + 1, :].rearrange("o (c p) -> (o c) p", p=128), w=["cwst"])
    P.dma(cwst[0:44, 3, :], dr["conv_b"][0:1, :].rearrange("o (c p) -> (o c) p", p=128), w=["cwst"])
    for kk in range(4):
        tr(bank(7, kk * 44, 44), cwst[0:44, kk, :], identf[0:44, 0:44], r=["cwst", "identf"], w=["b7"])
    cp("dve", cwT, bank(7, 0, 176).rearrange("p (a b) -> p a b", b=44), r=["b7"], w=["cwT"])
    WK = ["wupb0", "wupb1", "wupb2"]
    for c in range(NC5):
        sl = c % 2
        if c == 0:
            memset("pool", h1Tc[sl][:, :, 0:2], 0.0, w=["h1Tc%d" % sl])
            P.dma(h1Tc[sl][:, :, 2:258], h1T_v[:, :, 0:256], w=["h1Tc%d" % sl])
        else:
            P.dma(h1Tc[sl], h1T_v[:, :, c * 256 - 2:c * 256 + 256], w=["h1Tc%d" % sl])
        for i in range(22):
            for which in range(2):
                col0 = which * 2816 + i * 128
                cid = which * 22 + i
                bn = 4 + 2 * (i % 2) + which
                pb = bank(bn)
                for fc in range(8):
                    mm(pb[:, 0:258], w_up_bf[:, fc, col0:col0 + 128], h1Tc[sl][:, fc, :], fc == 0, fc == 7,
                       r=["h1Tc%d" % sl] + WK, w=["b%d" % bn])
                act(Aa[which], pb[:, 2:258], AF.Identity, r=["b%d" % bn, "cwT"], w=["Aa%d" % which],
                    scale=cwT[:, 2, cid:cid + 1], bias=cwT[:, 3, cid:cid + 1])
                stt(bb[which], pb[:, 1:257], cwT[:, 1, cid:cid + 1], Aa[which], ALU.mult, ALU.add,
                    r=["b%d" % bn, "Aa%d" % which, "cwT"], w=["bb%d" % which])
                stt(cc[which], pb[:, 0:256], cwT[:, 0, cid:cid + 1], bb[which], ALU.mult, ALU.add,
                    r=["b%d" % bn, "bb%d" % which, "cwT"], w=["cc%d" % which])
            act(sg, cc[0], AF.Silu, r=["cc0"], w=["sg"])
            tt("pool", G[i % 2], sg, cc[1], ALU.mult, r=["sg", "cc1"], w=["G%d" % (i % 2)])
            for sub in range(2):
                for half in range(2):
                    mm(bank(sub * 2 + half), G[i % 2][:, sub * 128:(sub + 1) * 128],
                       w_dn_bf[:, i, half * 512:(half + 1) * 512], i == 0, i == 21,
                       r=["G%d" % (i % 2), "wdnb0", "wdnb1"], w=["b%d" % (sub * 2 + half)])
        for sub in range(2):
            t_i = c * 2 + sub
            P.dma(h1r[sub], h1s[t_i * 128:(t_i + 1) * 128, :], w=["h1r%d" % sub])
            for half in range(2):
                stt(hpre2[:, half * 512:(half + 1) * 512], h1r[sub][:, half * 512:(half + 1) * 512], ALPHA,
                    bank(sub * 2 + half), ALU.mult, ALU.add, r=["h1r%d" % sub, "b%d" % (sub * 2 + half)], w=["l2_src"])
            layer_norm(hpre2, outt[sub], g2, b2, "l2", "g2", "b2")
            P.dma(y[t_i * 128:(t_i + 1) * 128, :], outt[sub], r=["l2_dst"], w=["y"])


_NC_CACHE = {}


def kernel(**inputs):
    S = int(np.asarray(inputs["x"]).shape[1])
    B = int(np.asarray(inputs["x"]).shape[0])
    if S not in _NC_CACHE:
        _NC_CACHE[S] = build(S)
    nc = _NC_CACHE[S]
    in_maps = [core_inputs(inputs, b, S) for b in range(B)]
    res = run_bass_kernel_spmd(nc, in_maps, core_ids=list(range(B)))
    out = np.stack([np.asarray(res.results[b]["y"], dtype=np.float32).reshape(S, 1024) for b in range(B)], axis=0)
    return out
```

```python
import contextlib
import math
import numpy as np
import ml_dtypes
import concourse.bass as bass
import concourse.mybir as mybir
from concourse.bass_utils import run_bass_kernel_spmd

F32 = mybir.dt.float32
BF16 = mybir.dt.bfloat16
I32 = mybir.dt.int32
AF = mybir.ActivationFunctionType
ALU = mybir.AluOpType

D_MODEL = 1024
D_IN = 2840
D_FF = 2816
LN_EPS = 1e-5
RMS_EPS = 1e-5
ALPHA = 2.0 ** 0.25
LAM_INIT = 0.2
ROPE_THETA = 500000.0

ENGS = ["pe", "act", "dve", "pool", "sp"]
_BANK_KEYS = set("b%d" % i for i in range(8))
NDMA = 8


class Prog:
    def __init__(self, nc):
        self.nc = nc
        self.ops = {e: [] for e in ENGS}
        self.known = {e: {} for e in ENGS}
        self.clock = {}
        self.cnt = {}
        self.lastw = {}
        self.readers = {}
        self.marked = set()
        self.dma_rr = {e: 0 for e in ENGS}

    def _deps(self, r, w):
        deps = set()
        for k in r:
            ev = self.lastw.get(k)
            if ev is not None:
                deps.add(ev)
        for k in w:
            ev = self.lastw.get(k)
            if ev is not None:
                deps.add(ev)
            for ev in self.readers.get(k, ()):
                deps.add(ev)
        return deps

    def _resolve(self, eng, deps):
        kn = self.known[eng]
        waits = []
        for (s, i) in sorted(deps, key=lambda t: (t[0], -t[1])):
            if eng == "pe" and s == "pe":
                continue
            if kn.get(s, 0) >= i:
                continue
            waits.append((s, i))
            self.marked.add((s, i))
            for s2, i2 in self.clock[(s, i)].items():
                if kn.get(s2, 0) < i2:
                    kn[s2] = i2
        return waits

    def _record(self, ev, r, w):
        for k in w:
            self.lastw[k] = ev
            self.readers[k] = []
        for k in r:
            if k in w:
                continue
            lst = self.readers.setdefault(k, [])
            for n, old in enumerate(lst):
                if old[0] == ev[0]:
                    lst[n] = ev
                    break
            else:
                lst.append(ev)

    def add(self, eng, fn, r=(), w=()):
        bank_r = [k for k in r if k in _BANK_KEYS and k not in w]
        if bank_r:
            w = list(w) + bank_r
        deps = self._deps(r, w)
        waits = self._resolve(eng, deps)
        idx = self.cnt.get(eng, 0) + 1
        self.cnt[eng] = idx
        ev = (eng, idx)
        ck = dict(self.known[eng])
        ck[eng] = idx
        self.clock[ev] = ck
        self.ops[eng].append(dict(fn=fn, waits=waits, ev=ev, dma=False))
        self._record(ev, r, w)
        return ev

    def dma(self, out, in_, r=(), w=(), q="sp", **kw):
        deps = self._deps(r, w)
        j = self.dma_rr[q]
        self.dma_rr[q] = (j + 1) % NDMA
        stream = "d_%s_%d" % (q, j)
        prev = self.cnt.get(stream, 0)
        if prev > 0:
            deps.add((stream, prev))
        waits = self._resolve(q, deps)
        idx = prev + 1
        self.cnt[stream] = idx
        ev = (stream, idx)
        ck = dict(self.known[q])
        ck[stream] = idx
        self.clock[ev] = ck

        def fn(e, out=out, in_=in_, kw=kw):
            return e.dma_start(out=out, in_=in_, **kw)
        self.ops[q].append(dict(fn=fn, waits=waits, ev=(q, 0), dma=True, dev=ev))
        self._record(ev, r, w)
        return ev

    def barrier(self):
        latest = [(s, i) for s, i in self.cnt.items() if i > 0]
        for e in ENGS:
            waits = self._resolve(e, set(latest))
            if waits:
                self.ops[e].append(dict(fn=None, waits=waits, ev=(e, 0), dma=False))
        self.lastw = {}
        self.readers = {}

    def emit(self):
        nc = self.nc
        streams = sorted(self.cnt.keys())
        rank = {}
        for s in streams:
            c = 0
            if s.startswith("d_"):
                for i in range(1, self.cnt[s] + 1):
                    rank[(s, i)] = 16 * i
            else:
                for i in range(1, self.cnt[s] + 1):
                    if (s, i) in self.marked:
                        c += 1
                        rank[(s, i)] = c
        with contextlib.ExitStack() as st:
            sems = {}
            for s in streams:
                sems[s] = st.enter_context(nc.semaphore("sem_" + s))
            block = st.enter_context(nc.Block())
            ops = self.ops
            marked = self.marked

            def run(e, name):
                for op in ops[name]:
                    for (s, i) in op["waits"]:
                        e.wait_ge(sems[s], rank[(s, i)])
                    if op["fn"] is None:
                        continue
                    ins = op["fn"](e)
                    if op["dma"]:
                        ins.then_inc(sems[op["dev"][0]], 16)
                    elif op["ev"] in marked:
                        ins.then_inc(sems[name], 1)

            @block.tensor
            def _(e):
                run(e, "pe")

            @block.scalar
            def _(e):
                run(e, "act")

            @block.vector
            def _(e):
                run(e, "dve")

            @block.gpsimd
            def _(e):
                run(e, "pool")

            @block.sync
            def _(e):
                run(e, "sp")


class Arena:
    def __init__(self, ap, nbytes):
        self.ap = ap
        self.nbytes = nbytes
        self.off = 0

    def alloc(self, shape, dtype):
        esz = 2 if dtype == BF16 else 4
        n = 1
        for s in shape:
            n *= s
        nb = n * esz
        self.off = (self.off + 63) // 64 * 64
        assert self.off + nb <= self.nbytes, ("arena overflow", self.off, nb, self.nbytes)
        a = self.ap[:, self.off // 2:(self.off + nb) // 2]
        self.off += nb
        if dtype != BF16:
            a = a.bitcast(dtype)
        if len(shape) == 2:
            a = a.rearrange("p (a b) -> p a b", b=shape[1])
        elif len(shape) == 3:
            a = a.rearrange("p (a b c) -> p a b c", b=shape[1], c=shape[2])
        return a

    def mark(self):
        return self.off

    def reset(self, m):
        self.off = m


def host_consts(S):
    NT = S // 128
    c = {}
    c["ident"] = np.eye(128, dtype=np.float32).astype(ml_dtypes.bfloat16)
    k = np.arange(128)[:, None]
    q = np.arange(128)[None, :]
    c["tri_le"] = (k <= q).astype(np.float32).astype(ml_dtypes.bfloat16)
    c["tri_gt"] = (k > q).astype(np.float32).astype(ml_dtypes.bfloat16)
    perm = np.zeros((128, 128), np.float32)
    invf = np.zeros((128, 1), np.float32)
    sgn = np.zeros((128, 1), np.float32)
    inv_freq = (1.0 / (ROPE_THETA ** (np.arange(8, dtype=np.float32) / np.float32(8)))).astype(np.float32)
    for m in range(128):
        mm = m % 64
        if mm < 8:
            perm[m + 8, m] = 1.0
            invf[m, 0] = inv_freq[mm]
            sgn[m, 0] = -1.0
        elif mm < 16:
            perm[m - 8, m] = 1.0
            invf[m, 0] = inv_freq[mm - 8]
            sgn[m, 0] = 1.0
    c["perm"] = perm.astype(ml_dtypes.bfloat16)
    cols = np.zeros((128, 4), np.float32)
    cols[:, 0:1] = invf
    cols[:, 1:2] = sgn * 6.28318
    cols[:, 2] = LN_EPS
    cols[:, 3] = 0.0
    c["cols"] = cols
    c["onehot"] = (np.arange(S)[None, :] // 64 == np.arange(64)[:, None]).astype(np.float32).astype(ml_dtypes.bfloat16)
    n_cmp = S // 16 - 1
    n_slc = S // 64
    NCT = (n_cmp + 127) // 128
    cs = np.arange(NCT * 128)[:, None] * 16
    ss = np.arange(64)[None, :] * 64
    ov = np.clip(np.minimum(cs + 32, ss + 64) - np.maximum(cs, ss), 0, None) / 32.0
    ov[n_cmp:, :] = 0.0
    c["ovl"] = ov.astype(np.float32).astype(ml_dtypes.bfloat16)
    force = np.zeros((128, NT, 64), np.float32)
    for qt in range(NT):
        for p in range(128):
            cur = (qt * 128 + p) // 64
            force[p, qt, 0] = 1e6
            force[p, qt, cur] = 1e6
            if cur >= 1:
                force[p, qt, cur - 1] = 1e6
    c["force"] = force
    c["val16"] = (16.0 * np.arange(128)[:, None] - np.arange(512)[None, :]).astype(np.float32)
    return c


CONST_SPECS = lambda S: [
    ("ident", [128, 128], BF16), ("tri_le", [128, 128], BF16), ("tri_gt", [128, 128], BF16),
    ("perm", [128, 128], BF16), ("cols", [128, 4], F32), ("onehot", [64, S], BF16),
    ("ovl", [((S // 16 - 1 + 127) // 128) * 128, 64], BF16), ("force", [128, S // 128, 64], F32),
    ("val16", [128, 512], F32),
]

PARAM_SPECS = [
    ("w_in", [1024, D_IN]), ("lambda_q1", [1, 64]), ("lambda_k1", [1, 64]), ("lambda_q2", [1, 64]),
    ("lambda_k2", [1, 64]), ("diff_norm_g", [1, 128]),
    ("cmp_pe_k", [32, 64]), ("cmp_w1_k", [2048, 128]), ("cmp_b1_k", [128, 1]), ("cmp_w2_k", [128, 64]),
    ("cmp_pe_v", [32, 64]), ("cmp_w1_v", [2048, 128]), ("cmp_b1_v", [128, 1]), ("cmp_w2_v", [128, 64]),
    ("w_out", [1024, 1024]), ("ln1_g", [1, 1024]), ("ln1_b", [1, 1024]),
    ("w_up", [1024, 2 * D_FF]), ("conv_w", [3, 2 * D_FF]), ("conv_b", [1, 2 * D_FF]),
    ("w_down", [D_FF, 1024]), ("ln2_g", [1, 1024]), ("ln2_b", [1, 1024]),
]


def _ns(d):
    import types
    return types.SimpleNamespace(**d)


def build(S, stop_after=None):
    assert S % 512 == 0
    NT = S // 128
    NQC = S // 512
    n_cmp = S // 16 - 1
    NCT = (n_cmp + 127) // 128
    nc = bass.Bass("TRN2", target_bir_lowering=False)
    dr = {}
    dr["x"] = nc.dram_tensor("x", [S, 1024], F32, kind="ExternalInput").ap()
    dr["pos"] = nc.dram_tensor("pos", [1, S], I32, kind="ExternalInput").ap()
    for name, shp in PARAM_SPECS:
        dr[name] = nc.dram_tensor(name, shp, F32, kind="ExternalInput").ap()
    for name, shp, dt in CONST_SPECS(S):
        dr[name] = nc.dram_tensor("c_" + name, shp, dt, kind="ExternalInput").ap()
    y = nc.dram_tensor("y", [S, 1024], F32, kind="ExternalOutput").ap()
    fm = nc.dram_tensor("s_fm", [16, 128, S], BF16, kind="Internal").ap()
    tm = nc.dram_tensor("s_tm", [S, 768], BF16, kind="Internal").ap()
    gt = nc.dram_tensor("s_gt", [S, 24], F32, kind="Internal").ap()
    mix = nc.dram_tensor("s_mix", [S, 1024], BF16, kind="Internal").ap()
    h1s = nc.dram_tensor("s_h1", [S, 1024], F32, kind="Internal").ap()
    h1T = nc.dram_tensor("s_h1T", [8, 128, S], BF16, kind="Internal").ap()

    ARENA_BYTES = 206 * 1024
    with contextlib.ExitStack() as st:
        arena_t = st.enter_context(nc.sbuf_tensor("arena", [128, ARENA_BYTES // 2], BF16))
        psum_t = st.enter_context(nc.psum_tensor("psum", [128, 4096], F32))
        P = Prog(nc)
        A = Arena(arena_t, ARENA_BYTES)

        def bank(b, c0=0, n=512):
            return psum_t[:, b * 512 + c0:b * 512 + c0 + n]

        def bank_bf(b):
            return psum_t[:, b * 512:(b + 1) * 512].bitcast(BF16)

        def mm(out, lhsT, rhs, start, stop, r, w):
            P.add("pe", lambda e: e.matmul(out, lhsT=lhsT, rhs=rhs, start=start, stop=stop), r=r, w=w)

        def tr(out, in_, ident, r, w):
            P.add("pe", lambda e: e.transpose(out=out, in_=in_, identity=ident), r=r, w=w)

        def act(out, in_, func, r, w, **kw):
            P.add("act", lambda e: e.activation(out=out, in_=in_, func=func, **kw), r=r, w=w)

        def cp(eng, out, in_, r, w):
            if eng == "act":
                P.add("act", lambda e: e.activation(out=out, in_=in_, func=AF.Copy), r=r, w=w)
            else:
                P.add(eng, lambda e: e.tensor_copy(out=out, in_=in_), r=r, w=w)

        def tt(eng, out, in0, in1, op, r, w):
            P.add(eng, lambda e: e.tensor_tensor(out=out, in0=in0, in1=in1, op=op), r=r, w=w)

        def ts(eng, out, in0, s1, s2, op0, op1, r, w):
            if op1 is None:
                P.add(eng, lambda e: e.tensor_scalar(out=out, in0=in0, scalar1=s1, scalar2=None, op0=op0), r=r, w=w)
            else:
                P.add(eng, lambda e: e.tensor_scalar(out=out, in0=in0, scalar1=s1, scalar2=s2, op0=op0, op1=op1), r=r, w=w)

        def stt(out, in0, scalar, in1, op0, op1, r, w):
            P.add("dve", lambda e: e.scalar_tensor_tensor(out=out, in0=in0, scalar=scalar, in1=in1, op0=op0, op1=op1), r=r, w=w)

        def recip(out, in_, r, w):
            P.add("dve", lambda e: e.reciprocal(out=out, in_=in_), r=r, w=w)

        def memset(eng, ap, val, w):
            P.add(eng, lambda e: e.memset(ap, val), r=(), w=w)

        ident = A.alloc([128], BF16)
        tri_le = A.alloc([128], BF16)
        tri_gt = A.alloc([128], BF16)
        perm = A.alloc([128], BF16)
        cols = A.alloc([4], F32)
        invf = cols[:, 0:1]
        sgn2pi = cols[:, 1:2]
        epsc = cols[:, 2:3]
        Ckc = A.alloc([NCT * 128], F32)
        Skc = A.alloc([NCT * 128], F32)
        neglam = A.alloc([1], F32)
        g08 = A.alloc([128], F32)
        for nm, t_ in (("ident", ident), ("tri_le", tri_le), ("tri_gt", tri_gt), ("perm", perm), ("cols", cols)):
            P.dma(t_, dr[nm][:, :], w=["c_" + nm])
        pm = A.mark()
        lq = A.alloc([4, 64], F32)
        for i, nm in enumerate(["lambda_q1", "lambda_k1", "lambda_q2", "lambda_k2"]):
            P.dma(lq[:, i, :], dr[nm][0:1, :].to_broadcast([128, 64]), w=["lq%d" % i])
        lt = A.alloc([2, 64], F32)
        ls = A.alloc([2], F32)
        tt("dve", lt[:, 0, :], lq[:, 0, :], lq[:, 1, :], ALU.mult, r=["lq0", "lq1"], w=["lt0"])
        tt("dve", lt[:, 1, :], lq[:, 2, :], lq[:, 3, :], ALU.mult, r=["lq2", "lq3"], w=["lt1"])
        P.add("dve", lambda e: e.reduce_sum(out=ls[:, 0:1], in_=lt[:, 0, :], axis=mybir.AxisListType.X), r=["lt0"], w=["ls0"])
        P.add("dve", lambda e: e.reduce_sum(out=ls[:, 1:2], in_=lt[:, 1, :], axis=mybir.AxisListType.X), r=["lt1"], w=["ls1"])
        act(ls, ls, AF.Exp, r=["ls0", "ls1"], w=["lse"])
        stt(neglam, ls[:, 1:2], -LAM_INIT, ls[:, 0:1], ALU.add, ALU.subtract, r=["lse"], w=["neglam"])
        P.dma(g08, dr["diff_norm_g"][0:1, :].to_broadcast([128, 128]), w=["g08"])
        ts("dve", g08, g08, 1.0 - LAM_INIT, None, ALU.mult, None, r=["g08"], w=["g08"])
        P.barrier()
        A.reset(pm)
        persist_mark = A.mark()

        w_in_bf = A.alloc([8, D_IN], BF16)
        xT = A.alloc([8, S], BF16)
        Ck = A.alloc([S], F32)
        Sk = A.alloc([S], F32)
        wst = A.alloc([D_IN], F32)
        xtf = [A.alloc([1024], F32) for _ in range(2)]
        xtb = [A.alloc([1024], BF16) for _ in range(2)]
        Abf = [A.alloc([512], BF16) for _ in range(2)]
        t1 = [A.alloc([512], F32) for _ in range(2)]
        t2 = [A.alloc([512], F32) for _ in range(2)]
        ob = [A.alloc([512], BF16) for _ in range(3)]
        vtok = [A.alloc([768], BF16) for _ in range(2)]
        gts = [A.alloc([24], F32) for _ in range(2)]

        xflat = xT.rearrange("p a b -> p (a b)").bitcast(F32)
        tv = xflat[:, 0:S]
        tw = xflat[:, S:2 * S]
        tki = xflat[:, 2 * S:3 * S].bitcast(I32)
        tg = xflat[:, 3 * S:4 * S]
        tpi = xflat[:, 3 * S:4 * S].bitcast(I32)
        P.dma(tpi, dr["pos"][0:1, :].to_broadcast([128, S]), w=["tg"])
        cp("dve", tw, tpi, r=["tg"], w=["tw"])
        ts("dve", tv, tw, invf, 1.0 / (2.0 * math.pi), ALU.mult, ALU.mult, r=["tw"], w=["tv"])

        def frac_to(dst, scale):
            cp("dve", tki, tv, r=["tv"], w=["tki"])
            cp("dve", tw, tki, r=["tki"], w=["tw"])
            tt("dve", tw, tv, tw, ALU.subtract, r=["tv", "tw"], w=["tw"])
            ts("dve", tg, tw, 0.5, None, ALU.is_gt, None, r=["tw"], w=["tg"])
            tt("dve", tw, tw, tg, ALU.subtract, r=["tw", "tg"], w=["tw"])
            ts("dve", tg, tw, -0.5, None, ALU.is_lt, None, r=["tw"], w=["tg"])
            tt("dve", tw, tw, tg, ALU.add, r=["tw", "tg"], w=["tw"])
            act(dst, tw, AF.Sin, r=["tw"], w=["tbl"], scale=scale)

        frac_to(Sk, sgn2pi)
        ts("dve", tv, tv, 0.25, None, ALU.add, None, r=["tv", "tbl"], w=["tv"])
        frac_to(Ck, 6.28318)
        ckv = Ck[:, 16:S].rearrange("p (n s) -> p n s", s=16)
        skv = Sk[:, 16:S].rearrange("p (n s) -> p n s", s=16)
        memset("dve", Ckc, 0.0, w=["Ckc"])
        memset("dve", Skc, 0.0, w=["Skc"])
        cp("dve", Ckc[:, 0:n_cmp], ckv[:, :, 15], r=["tbl", "Ckc"], w=["Ckc"])
        cp("dve", Skc[:, 0:n_cmp], skv[:, :, 15], r=["tbl", "Skc"], w=["Skc"])
        P.barrier()

        for fc in range(8):
            P.dma(wst, dr["w_in"][fc * 128:(fc + 1) * 128, :], w=["wst"])
            cp("dve", w_in_bf[:, fc, 0:1420], wst[:, 0:1420], r=["wst"], w=["winb%d" % fc])
            cp("pool", w_in_bf[:, fc, 1420:D_IN], wst[:, 1420:D_IN], r=["wst"], w=["winc%d" % fc])
        for t_i in range(NT):
            sl = t_i % 2
            P.dma(xtf[sl], dr["x"][t_i * 128:(t_i + 1) * 128, :], w=["xtf%d" % sl])
            cp("pool", xtb[sl], xtf[sl], r=["xtf%d" % sl], w=["xtb%d" % sl])
            pst = bank_bf(7)
            for fc in range(8):
                tr(pst[:, fc * 128:(fc + 1) * 128], xtb[sl][:, fc * 128:(fc + 1) * 128], ident,
                   r=["xtb%d" % sl], w=["b7"])
            cp("act", xT[:, :, t_i * 128:(t_i + 1) * 128], pst.rearrange("p (a b) -> p a b", b=128),
               r=["b7"], w=["xT%d" % t_i])
        WIN_KEYS = ["winb%d" % i for i in range(8)] + ["winc%d" % i for i in range(8)]
        CH_COLS = [0, 128, 256, 384, 512, 640, 768, 896, 1536, 1664, 1792, 1920, 2048, 2176, 2304, 2560]
        CH_ROPE = [True] * 12 + [False, False, True, True]
        def fm_mm(it):
            tc, ch = it // 16, it % 16
            xkeys = ["xT%d" % (tc * 4 + i) for i in range(4)]
            b = 4 + (it % 2)
            c0 = CH_COLS[ch]
            for fc in range(8):
                mm(bank(b), w_in_bf[:, fc, c0:c0 + 128], xT[:, fc, tc * 512:(tc + 1) * 512],
                   fc == 0, fc == 7, r=xkeys + WIN_KEYS, w=["b%d" % b])

        def fm_post(it):
            tc, ch = it // 16, it % 16
            b = 4 + (it % 2)
            s2 = it % 2
            s3 = it % 3
            if CH_ROPE[ch]:
                cp("act", Abf[s2], bank(b), r=["b%d" % b], w=["Abf%d" % s2])
                mm(bank(6), perm, Abf[s2], True, True, r=["Abf%d" % s2], w=["b6"])
                tt("dve", t1[s2], bank(b), Ck[:, tc * 512:(tc + 1) * 512], ALU.mult, r=["b%d" % b], w=["t1%d" % s2])
                tt("dve", t2[s2], bank(6), Sk[:, tc * 512:(tc + 1) * 512], ALU.mult, r=["b6"], w=["t2%d" % s2])
                tt("pool", ob[s3], t1[s2], t2[s2], ALU.add, r=["t1%d" % s2, "t2%d" % s2], w=["ob%d" % s3])
            else:
                cp("act", ob[s3], bank(b), r=["b%d" % b], w=["ob%d" % s3])
            P.dma(fm[ch, :, tc * 512:(tc + 1) * 512], ob[s3], r=["ob%d" % s3], w=["fm%d" % ch])

        NIT = NQC * 16
        for it in range(NIT + 1):
            if it < NIT:
                fm_mm(it)
            if it >= 1:
                fm_post(it - 1)
        for t_i in range(NT):
            sl = t_i % 2
            ba = 0 + sl
            bb = 2 + sl
            xk = ["xT%d" % t_i]
            for fc in range(8):
                mm(bank(ba), xT[:, fc, t_i * 128:(t_i + 1) * 128], w_in_bf[:, fc, 1024:1536], fc == 0, fc == 7,
                   r=xk + WIN_KEYS, w=["b%d" % ba])
            for fc in range(8):
                mm(bank(bb, 0, 128), xT[:, fc, t_i * 128:(t_i + 1) * 128], w_in_bf[:, fc, 2432:2560], fc == 0, fc == 7,
                   r=xk + WIN_KEYS, w=["b%d" % bb])
            for fc in range(8):
                mm(bank(bb, 128, 152), xT[:, fc, t_i * 128:(t_i + 1) * 128], w_in_bf[:, fc, 2688:2840], fc == 0, fc == 7,
                   r=xk + WIN_KEYS, w=["b%d" % bb])
            cp("act", vtok[sl][:, 0:512], bank(ba), r=["b%d" % ba], w=["vtokA%d" % sl])
            cp("dve", vtok[sl][:, 512:768], bank(bb, 0, 256), r=["b%d" % bb], w=["vtokB%d" % sl])
            act(gts[sl], bank(bb, 256, 24), AF.Sigmoid, r=["b%d" % bb], w=["gts%d" % sl])
            P.dma(tm[t_i * 128:(t_i + 1) * 128, :], vtok[sl], r=["vtokA%d" % sl, "vtokB%d" % sl], w=["tm"])
            P.dma(gt[t_i * 128:(t_i + 1) * 128, :], gts[sl], r=["gts%d" % sl], w=["gt"])
        P.barrier()
        A.reset(persist_mark)

        ET = [A.alloc([512], BF16) for _ in range(3)]
        att_mark = A.mark()
        st_cnt = [0]

        def attn_tile(lhsT, rhs_fn, kdeps, sub0, sub1, mask, mask_sub, v1, v1deps, acc_w, first_fn, last_fn, accn):
            i = st_cnt[0]
            st_cnt[0] += 1
            sb = 4 + (i % 3)
            es = i % 3
            ncols = (sub1 - sub0) * 128
            mm(bank(sb, 0, ncols), lhsT, rhs_fn(sub0, sub1), True, True, r=kdeps, w=["b%d" % sb])
            act(ET[es][:, 0:ncols], bank(sb, 0, ncols), AF.Exp, r=["b%d" % sb], w=["ET%d" % es], scale=0.125)
            if mask is not None:
                if callable(mask):
                    mask(ET[es], ncols, "ET%d" % es)
                else:
                    c_ = (mask_sub - sub0) * 128
                    tt("pool", ET[es][:, c_:c_ + 128], ET[es][:, c_:c_ + 128], mask, ALU.mult,
                       r=["ET%d" % es], w=["ET%d" % es])
            for sub in range(sub0, sub1):
                c_ = (sub - sub0) * 128
                mm(bank(sub, 0, accn), ET[es][:, c_:c_ + 128], v1, first_fn(sub), last_fn(sub),
                   r=["ET%d" % es] + v1deps, w=["b%d" % sub])

        pend = []
        LOOK = 2

        def q_flush(keep=0):
            while True:
                nav = sum(1 for k_, _f in pend if k_ == "av")
                if nav <= keep:
                    break
                pend.pop(0)[1]()
            if keep == 0:
                while pend:
                    pend.pop(0)[1]()
            else:
                while pend and pend[0][0] == "fin":
                    pend.pop(0)[1]()

        def q_tile(score_fn, av_fn):
            score_fn()
            pend.append(("av", av_fn))
            q_flush(LOOK)

        def q_fin(fn):
            pend.append(("fin", fn))

        if stop_after != "p1":
            KT = [A.alloc([S], BF16) for _ in range(2)]
            QT = [A.alloc([S], BF16) for _ in range(2)]
            V1 = [A.alloc([NT, 129], BF16) for _ in range(2)]
            O1 = A.alloc([4, 129], F32)
            r1 = A.alloc([4], F32)
            r2 = A.alloc([4], F32)
            tmpo = A.alloc([128], F32)
            od = A.alloc([128], F32)
            junk = A.alloc([128], F32)
            ssq = A.alloc([4], F32)
            odn = [A.alloc([4, 128], BF16) for _ in range(2)]
            for kb in range(2):
                memset("pool", V1[kb][:, :, 128:129], 1.0, w=["V1o%d" % kb])

            def load_head(h):
                kb = h % 2
                P.dma(KT[kb], fm[4 + h, :, :], w=["KT%d" % kb])
                P.dma(QT[kb], fm[h, :, :], w=["QT%d" % kb])
                vsrc = tm[:, h * 128:(h + 1) * 128].rearrange("(n p) c -> p n c", p=128)
                for n0 in range(0, NT, 8):
                    n1 = min(NT, n0 + 8)
                    P.dma(V1[kb][:, n0:n1, 0:128], vsrc[:, n0:n1, :], w=["V1d%d_%d" % (kb, n0)])

            load_head(0)
            fin = 0

            def p2_tile(kb, rows, qc, kt, v1deps):
                j = kt - 4 * qc
                sub0 = max(j, 0)
                i = st_cnt[0]
                st_cnt[0] += 1
                sb = 4 + (i % 3)
                es = i % 3
                ncols = (4 - sub0) * 128

                def score():
                    mm(bank(sb, 0, ncols), KT[kb][rows, kt * 128:(kt + 1) * 128],
                       QT[kb][rows, qc * 512 + sub0 * 128:(qc + 1) * 512], True, True,
                       r=["KT%d" % kb, "QT%d" % kb], w=["b%d" % sb])
                    act(ET[es][:, 0:ncols], bank(sb, 0, ncols), AF.Exp, r=["b%d" % sb], w=["ET%d" % es], scale=0.125)
                    if j >= 0:
                        tt("pool", ET[es][:, 0:128], ET[es][:, 0:128], tri_le, ALU.mult,
                           r=["ET%d" % es], w=["ET%d" % es])

                def av():
                    for sub in range(sub0, 4):
                        c_ = (sub - sub0) * 128
                        mm(bank(sub, 0, 129), ET[es][:, c_:c_ + 128], V1[kb][:, kt, :],
                           kt == 0, kt == 4 * qc + sub, r=["ET%d" % es] + v1deps, w=["b%d" % sub])
                q_tile(score, av)

            def p2_fin0():
                for sub in range(4):
                    cp("act", O1[:, sub, :], bank(sub, 0, 129), r=["b%d" % sub], w=["O1_%d" % sub])

            def p2_fin1(h, qc, fs):
                recip(r1, O1[:, :, 128], r=["O1_%d" % s_ for s_ in range(4)], w=["r1"])
                for sub in range(4):
                    recip(r2[:, sub:sub + 1], bank(sub, 128, 1), r=["b%d" % sub], w=["r2"])
                    tt("dve", r2[:, sub:sub + 1], r2[:, sub:sub + 1], neglam, ALU.mult, r=["r2"], w=["r2"])
                    ts("dve", tmpo, O1[:, sub, 0:128], r1[:, sub:sub + 1], None, ALU.mult, None,
                       r=["O1_%d" % sub, "r1"], w=["tmpo"])
                    stt(od, bank(sub, 0, 128), r2[:, sub:sub + 1], tmpo, ALU.mult, ALU.add,
                        r=["b%d" % sub, "r2", "tmpo"], w=["od"])
                    act(junk, od, AF.Square, r=["od"], w=["junk", "ssq"], accum_out=ssq[:, sub:sub + 1])
                    act(ssq[:, sub:sub + 1], ssq[:, sub:sub + 1], AF.Sqrt, r=["ssq"], w=["ssq"],
                        scale=1.0 / 128.0, bias=epsc)
                    recip(ssq[:, sub:sub + 1], ssq[:, sub:sub + 1], r=["ssq"], w=["ssq"])
                    stt(odn[fs][:, sub, :], od, ssq[:, sub:sub + 1], g08, ALU.mult, ALU.mult,
                        r=["od", "ssq"], w=["odn%d" % fs])
                P.dma(mix[qc * 512:(qc + 1) * 512, h * 128:(h + 1) * 128].rearrange("(s p) c -> p s c", p=128),
                      odn[fs], r=["odn%d" % fs], w=["mix"])

            for h in range(4):
                kb = h % 2
                q_flush(0)
                if h + 1 < 4:
                    load_head(h + 1)
                v1deps = ["V1o%d" % kb] + ["V1d%d_%d" % (kb, n0) for n0 in range(0, NT, 8)]
                for qc in range(NQC):
                    for c in range(2):
                        rows = slice(64 * c, 64 * c + 64)
                        for kt in range(4 * qc + 4):
                            p2_tile(kb, rows, qc, kt, v1deps)
                        if c == 0:
                            q_fin(p2_fin0)
                    fs = fin % 2
                    fin += 1
                    q_fin(lambda h=h, qc=qc, fs=fs: p2_fin1(h, qc, fs))
            q_flush(0)
            P.barrier()
        A.reset(att_mark)

        if stop_after not in ("p1", "p2"):
            build_nsa(_ns(locals()))
        P.barrier()
        A.reset(persist_mark)
        if stop_after not in ("p1", "p2", "p3"):
            build_tail(_ns(locals()))
        P.barrier()
        P.emit()
    return nc


def core_inputs(inputs, b, S):
    c = host_consts(S)
    m = {"x": np.ascontiguousarray(inputs["x"][b], dtype=np.float32),
         "pos": np.ascontiguousarray(inputs["positions"][b:b + 1], dtype=np.int32)}
    for name, shp in PARAM_SPECS:
        m[name] = np.ascontiguousarray(np.asarray(inputs[name][0], dtype=np.float32).reshape(shp))
    for name, shp, dt in CONST_SPECS(S):
        m["c_" + name] = np.ascontiguousarray(c[name])
    return m


def build_nsa(ns):
    P, A, S, NT, NQC, n_cmp, NCT = ns.P, ns.A, ns.S, ns.NT, ns.NQC, ns.n_cmp, ns.NCT
    bank, bank_bf, mm, tr, act, cp, tt, ts, stt, recip, memset = (ns.bank, ns.bank_bf, ns.mm, ns.tr, ns.act, ns.cp,
                                                                   ns.tt, ns.ts, ns.stt, ns.recip, ns.memset)
    fm, tm, gt, mix, dr = ns.fm, ns.tm, ns.gt, ns.mix, ns.dr
    ET, st_cnt = ns.ET, ns.st_cnt
    ident, tri_le, tri_gt, perm, Ckc, Skc = ns.ident, ns.tri_le, ns.tri_gt, ns.perm, ns.Ckc, ns.Skc
    NCP = NCT * 128

    KTs = A.alloc([2, S], BF16)
    KTw = A.alloc([2, S], BF16)
    QTa = A.alloc([2, 8, 512], BF16)
    V1s = A.alloc([2, NT, 65], BF16)
    V1w = A.alloc([2, NT, 65], BF16)
    kcmpT = A.alloc([2, NCP], BF16)
    V1c = A.alloc([2, NCT, 129], BF16)
    force = A.alloc([NT, 64], F32)
    val16 = A.alloc([512], F32)
    gtile = A.alloc([2, 4, 24], F32)
    on_acc = A.alloc([4, 512], F32)
    on_bf = A.alloc([2, 4, 512], BF16)
    imp = A.alloc([4, 2, 64], F32)
    imp2 = A.alloc([64], F32)
    imp3 = A.alloc([64], F32)
    m8 = A.alloc([8], F32)
    m8b = A.alloc([8], F32)
    negpad = A.alloc([4, 128], BF16)
    rc = A.alloc([4], F32)
    gc = A.alloc([4], F32)

    P.dma(force, dr["force"][:, :, :], w=["c_force"])
    P.dma(val16, dr["val16"][:, :], w=["c_val16"])
    memset("pool", negpad, 0.0, w=["negpad"])
    for g in range(2):
        P.dma(KTs[0:64, g, :], fm[14, 64 * g:64 * g + 64, :], w=["KTs"])
        P.dma(KTs[64:128, g, :], dr["onehot"][:, :], w=["KTs"])
        P.dma(KTw[0:64, g, :], fm[15, 64 * g:64 * g + 64, :], w=["KTw"])
        memset("pool", V1s[:, g, :, 64:65], 1.0, w=["V1s"])
        memset("pool", V1w[:, g, :, 64:65], 1.0, w=["V1w"])
        vs_src = tm[:, 512 + 64 * g:512 + 64 * g + 64].rearrange("(n p) c -> p n c", p=128)
        vw_src = tm[:, 640 + 64 * g:640 + 64 * g + 64].rearrange("(n p) c -> p n c", p=128)
        for n0 in range(0, NT, 8):
            n1 = min(NT, n0 + 8)
            P.dma(V1s[:, g, n0:n1, 0:64], vs_src[:, n0:n1, :], w=["V1s"])
            P.dma(V1w[:, g, n0:n1, 0:64], vw_src[:, n0:n1, :], w=["V1w"])
        memset("pool", V1c[:, g, :, 64:65], 1.0, w=["V1c"])
        P.dma(V1c[:, g, :, 65:129], dr["ovl"].rearrange("(t p) j -> p t j", p=128), w=["V1c"])

    cm = A.mark()
    srcT = A.alloc([2, 2, S], BF16)
    w1st = A.alloc([32, 128], F32)
    w1bf = A.alloc([2, 32, 128], BF16)
    pest = A.alloc([64], F32)
    pebf = A.alloc([64], BF16)
    peT = A.alloc([2, 32], BF16)
    w2st = A.alloc([64], F32)
    w2bf = A.alloc([2, 64], BF16)
    b1c = A.alloc([2], F32)
    b1e = A.alloc([2], F32)
    z = A.alloc([NCP], F32)
    z2 = A.alloc([NCP], F32)
    zu = A.alloc([NCP], F32)
    hidb = A.alloc([NCP], BF16)
    Ab = A.alloc([NCP], BF16)
    ct1 = A.alloc([NCP], F32)
    ct2 = A.alloc([NCP], F32)
    memset("pool", hidb, 0.0, w=["hidb"])
    for wi, (nw1, npe, nb1, nw2, chn) in enumerate([("cmp_w1_k", "cmp_pe_k", "cmp_b1_k", "cmp_w2_k", 12),
                                                    ("cmp_w1_v", "cmp_pe_v", "cmp_b1_v", "cmp_w2_v", 13)]):
        for g in range(2):
            P.dma(srcT[0:64, wi, g, :], fm[chn, 64 * g:64 * g + 64, :], w=["srcT%d" % wi])
        P.dma(w1st[0:64], dr[nw1].rearrange("(l d) h -> d l h", d=64), w=["w1st"])
        cp("pool", w1bf[0:64, wi], w1st[0:64], r=["w1st"], w=["w1bf%d" % wi])
        P.dma(pest[0:32, :], dr[npe][:, :], w=["pest"])
        cp("dve", pebf[0:32, :], pest[0:32, :], r=["pest"], w=["pebf"])
        tr(bank_bf(7)[0:64, 0:32], pebf[0:32, 0:64], ident[0:32, 0:32], r=["pebf"], w=["b7"])
        cp("dve", peT[0:64, wi, :], bank_bf(7)[0:64, 0:32], r=["b7"], w=["peT%d" % wi])
        P.dma(w2st, dr[nw2][:, :], w=["w2st"])
        cp("dve", w2bf[:, wi, :], w2st, r=["w2st"], w=["w2bf%d" % wi])
        P.dma(b1c[:, wi:wi + 1], dr[nb1][:, :], w=["b1c%d" % wi])
        for l in range(32):
            mm(bank(5, 0, 1), w1bf[0:64, wi, l, :], peT[0:64, wi, l:l + 1], l == 0, l == 31,
               r=["w1bf%d" % wi, "peT%d" % wi], w=["b5"])
        tt("dve", b1e[:, wi:wi + 1], bank(5, 0, 1), b1c[:, wi:wi + 1], ALU.add, r=["b5", "b1c%d" % wi], w=["b1e%d" % wi])
        for g in range(2):
            v16 = srcT[0:64, wi, g, :].rearrange("p (n s) -> p n s", s=16)
            for l in range(32):
                mm(bank(4, 0, n_cmp), w1bf[0:64, wi, l, :], v16[:, l // 16:l // 16 + n_cmp, l % 16], l == 0, l == 31,
                   r=["w1bf%d" % wi, "srcT%d" % wi], w=["b4"])
            act(z[:, 0:n_cmp], bank(4, 0, n_cmp), AF.Identity, r=["b4", "b1e%d" % wi], w=["z"], bias=b1e[:, wi:wi + 1])
            tt("dve", z2[:, 0:n_cmp], z[:, 0:n_cmp], z[:, 0:n_cmp], ALU.mult, r=["z"], w=["z2"])
            ts("dve", z2[:, 0:n_cmp], z2[:, 0:n_cmp], 0.044715, 1.0, ALU.mult, ALU.add, r=["z2"], w=["z2"])
            tt("dve", zu[:, 0:n_cmp], z2[:, 0:n_cmp], z[:, 0:n_cmp], ALU.mult, r=["z2", "z"], w=["zu"])
            act(zu[:, 0:n_cmp], zu[:, 0:n_cmp], AF.Tanh, r=["zu"], w=["zu"], scale=0.7978845608028654)
            ts("dve", zu[:, 0:n_cmp], zu[:, 0:n_cmp], 1.0, 0.5, ALU.add, ALU.mult, r=["zu"], w=["zu"])
            tt("dve", hidb[:, 0:n_cmp], zu[:, 0:n_cmp], z[:, 0:n_cmp], ALU.mult, r=["zu", "z", "hidb"], w=["hidb"])
            if wi == 0:
                mm(bank(5, 0, NCP)[0:64, :], w2bf[:, wi, :], hidb, True, True, r=["hidb", "w2bf0"], w=["b5"])
                cp("act", Ab[0:64, :], bank(5, 0, NCP)[0:64, :], r=["b5"], w=["Ab"])
                mm(bank(6, 0, NCP)[0:64, :], perm[0:64, 0:64], Ab[0:64, :], True, True, r=["Ab"], w=["b6"])
                tt("dve", ct1[0:64, :], bank(5, 0, NCP)[0:64, :], Ckc[0:64, :], ALU.mult, r=["b5"], w=["ct1"])
                tt("dve", ct2[0:64, :], bank(6, 0, NCP)[0:64, :], Skc[0:64, :], ALU.mult, r=["b6"], w=["ct2"])
                tt("pool", kcmpT[0:64, g, :], ct1[0:64, :], ct2[0:64, :], ALU.add, r=["ct1", "ct2"], w=["kcmpT"])
            else:
                for ntl in range(NCT):
                    mm(bank(5, 0, 64), hidb[:, ntl * 128:(ntl + 1) * 128], w2bf[:, wi, :], True, True,
                       r=["hidb", "w2bf1"], w=["b5"])
                    cp("act", V1c[:, g, ntl, 0:64], bank(5, 0, 64), r=["b5", "V1c"], w=["V1c"])
    A.reset(cm)

    def load_q(qc):
        qs = qc % 2
        for h in range(8):
            P.dma(QTa[0:64, qs, h, :], fm[8 + h // 2, 64 * (h % 2):64 * (h % 2) + 64, qc * 512:(qc + 1) * 512],
                  w=["Qd%d_%d" % (qs, h)])
        P.dma(gtile[:, qs], gt[qc * 512:(qc + 1) * 512, :].rearrange("(s p) c -> p s c", p=128), w=["gat%d" % qs])

    def run_tiles(tiles, lhs_fn, rhs_fn, kdeps, v_fn, vdeps, accn):
        first = {}
        last = {}
        for ti, tl in enumerate(tiles):
            for sub in range(tl[1], tl[2]):
                first.setdefault(sub, ti)
                last[sub] = ti

        def one(ti, kt, sub0, sub1, mask, msub, maskfn):
            i = st_cnt[0]
            st_cnt[0] += 1
            sb = 4 + (i % 3)
            es = i % 3
            ncols = (sub1 - sub0) * 128

            def score():
                mm(bank(sb, 0, ncols), lhs_fn(kt), rhs_fn(sub0, sub1), True, True, r=kdeps, w=["b%d" % sb])
                act(ET[es][:, 0:ncols], bank(sb, 0, ncols), AF.Exp, r=["b%d" % sb], w=["ET%d" % es], scale=0.125)
                if mask is not None:
                    c_ = (msub - sub0) * 128
                    tt("pool", ET[es][:, c_:c_ + 128], ET[es][:, c_:c_ + 128], mask, ALU.mult,
                       r=["ET%d" % es], w=["ET%d" % es])
                if maskfn is not None:
                    maskfn(ET[es], "ET%d" % es)

            def av():
                for sub in range(sub0, sub1):
                    c_ = (sub - sub0) * 128
                    mm(bank(sub, 0, accn), ET[es][:, c_:c_ + 128], v_fn(kt), first[sub] == ti, last[sub] == ti,
                       r=["ET%d" % es] + vdeps, w=["b%d" % sub])
            ns.q_tile(score, av)

        for ti, (kt, sub0, sub1, mask, msub, maskfn) in enumerate(tiles):
            one(ti, kt, sub0, sub1, mask, msub, maskfn)

    load_q(0)
    for qc in range(NQC):
        qs = qc % 2
        if qc + 1 < NQC:
            load_q(qc + 1)
        for h in range(8):
            g = h // 4
            r_ = h % 4
            tiles = []
            for ntl in range(NCT):
                if 2048 * ntl + 31 > 512 * qc + 511:
                    continue
                full = (16 * (128 * ntl + 127) + 31 <= 512 * qc) and (128 * ntl + 127 < n_cmp)
                if full:
                    tiles.append((ntl, 0, 4, None, 0, None))
                else:
                    thr = float(512 * qc - 2048 * ntl - 31)

                    def mf(et, key, thr=thr):
                        stt(et, val16, thr, et, ALU.is_le, ALU.mult, r=[key, "c_val16"], w=[key])
                    tiles.append((ntl, 0, 4, None, 0, mf))
            run_tiles(tiles,
                      lambda kt, g=g: kcmpT[0:64, g, kt * 128:(kt + 1) * 128],
                      lambda a, b_, qs=qs, h=h: QTa[0:64, qs, h, a * 128:b_ * 128],
                      ["kcmpT", "Qd%d_%d" % (qs, h)],
                      lambda kt, g=g: V1c[:, g, kt, :], ["V1c"], 129)
            def _fin(h=h, g=g, qs=qs, r_=r_):
                hc = slice(h * 64, (h + 1) * 64)
                for sub in range(4):
                    ts("dve", rc[:, sub:sub + 1], bank(sub, 64, 1), 1e-30, None, ALU.max, None, r=["b%d" % sub], w=["rc"])
                    recip(rc[:, sub:sub + 1], rc[:, sub:sub + 1], r=["rc"], w=["rc"])
                    tt("dve", gc[:, sub:sub + 1], rc[:, sub:sub + 1], gtile[:, qs, sub, 3 * h:3 * h + 1], ALU.mult,
                       r=["rc", "gat%d" % qs], w=["gc"])
                    ts("dve", on_acc[:, sub, hc], bank(sub, 0, 64), gc[:, sub:sub + 1], None, ALU.mult, None,
                       r=["b%d" % sub, "gc"], w=["on_acc%d" % h])
                    if r_ == 0:
                        ts("dve", imp[:, sub, g, :], bank(sub, 65, 64), rc[:, sub:sub + 1], None, ALU.mult, None,
                           r=["b%d" % sub, "rc"], w=["imp%d" % g])
                    else:
                        stt(imp[:, sub, g, :], bank(sub, 65, 64), rc[:, sub:sub + 1], imp[:, sub, g, :], ALU.mult, ALU.add,
                            r=["b%d" % sub, "rc", "imp%d" % g], w=["imp%d" % g])
            ns.q_fin(_fin)
        ns.q_flush(0)
        for g in range(2):
            for sub in range(4):
                qt = 4 * qc + sub
                tt("dve", imp2, imp[:, sub, g, :], force[:, qt, :], ALU.add, r=["imp%d" % g, "c_force"], w=["imp2"])
                P.add("dve", lambda e: e.max(out=m8, in_=imp2), r=["imp2"], w=["m8"])
                P.add("dve", lambda e: e.match_replace(out=imp3, in_to_replace=m8, in_values=imp2, imm_value=-1e9),
                      r=["imp2", "m8"], w=["imp3"])
                P.add("dve", lambda e: e.max(out=m8b, in_=imp3), r=["imp3"], w=["m8b"])
                ts("dve", negpad[:, sub, 64:128], imp2, m8b[:, 7:8], -30000.0, ALU.is_lt, ALU.mult,
                   r=["imp2", "m8b", "negpad"], w=["negpad"])
                mm(bank(7, sub * 128, 128), negpad[:, sub, :], ident, True, True, r=["negpad"], w=["b7"])
            for r_ in range(4):
                h = 4 * g + r_
                cp("act" if r_ % 2 == 0 else "dve", QTa[64:128, qs, h, :], bank(7)[64:128, :], r=["b7"],
                   w=["Qm%d_%d" % (qs, h)])
        for h in range(8):
            g = h // 4
            tiles = []
            for j in range(4):
                kt = 4 * qc - 4 + j
                if kt >= 0:
                    tiles.append((kt, 0, j + 1, tri_gt, j, None))
            for j in range(4):
                tiles.append((4 * qc + j, j, 4, tri_le, j, None))
            run_tiles(tiles,
                      lambda kt, g=g: KTw[0:64, g, kt * 128:(kt + 1) * 128],
                      lambda a, b_, qs=qs, h=h: QTa[0:64, qs, h, a * 128:b_ * 128],
                      ["KTw", "Qd%d_%d" % (qs, h)],
                      lambda kt, g=g: V1w[:, g, kt, :], ["V1w"], 65)
            def _fin(h=h, g=g, qs=qs):
                hc = slice(h * 64, (h + 1) * 64)
                for sub in range(4):
                    recip(rc[:, sub:sub + 1], bank(sub, 64, 1), r=["b%d" % sub], w=["rc"])
                    tt("dve", gc[:, sub:sub + 1], rc[:, sub:sub + 1], gtile[:, qs, sub, 3 * h + 2:3 * h + 3], ALU.mult,
                       r=["rc", "gat%d" % qs], w=["gc"])
                    stt(on_acc[:, sub, hc], bank(sub, 0, 64), gc[:, sub:sub + 1], on_acc[:, sub, hc], ALU.mult, ALU.add,
                        r=["b%d" % sub, "gc", "on_acc%d" % h], w=["on_acc%d" % h])
            ns.q_fin(_fin)
        for h in range(8):
            g = h // 4
            tiles = []
            for kt in range(4 * qc + 4):
                j = kt - 4 * qc
                if j >= 0:
                    tiles.append((kt, j, 4, tri_le, j, None))
                else:
                    tiles.append((kt, 0, 4, None, 0, None))
            run_tiles(tiles,
                      lambda kt, g=g: KTs[:, g, kt * 128:(kt + 1) * 128],
                      lambda a, b_, qs=qs, h=h: QTa[:, qs, h, a * 128:b_ * 128],
                      ["KTs", "Qd%d_%d" % (qs, h), "Qm%d_%d" % (qs, h)],
                      lambda kt, g=g: V1s[:, g, kt, :], ["V1s"], 65)
            def _fin(h=h, g=g, qs=qs):
                hc = slice(h * 64, (h + 1) * 64)
                for sub in range(4):
                    recip(rc[:, sub:sub + 1], bank(sub, 64, 1), r=["b%d" % sub], w=["rc"])
                    tt("dve", gc[:, sub:sub + 1], rc[:, sub:sub + 1], gtile[:, qs, sub, 3 * h + 1:3 * h + 2], ALU.mult,
                       r=["rc", "gat%d" % qs], w=["gc"])
                    stt(on_bf[:, qs, sub, hc], bank(sub, 0, 64), gc[:, sub:sub + 1], on_acc[:, sub, hc], ALU.mult, ALU.add,
                        r=["b%d" % sub, "gc", "on_acc%d" % h], w=["on_bf%d" % qs])
            ns.q_fin(_fin)
        ns.q_flush(0)
        P.dma(mix[qc * 512:(qc + 1) * 512, 512:1024].rearrange("(s p) c -> p s c", p=128), on_bf[:, qs],
              r=["on_bf%d" % qs], w=["mix"])


def build_tail(ns):
    P, A, S, NT = ns.P, ns.A, ns.S, ns.NT
    bank, bank_bf, mm, tr, act, cp, tt, ts, stt, recip, memset = (ns.bank, ns.bank_bf, ns.mm, ns.tr, ns.act, ns.cp,
                                                                   ns.tt, ns.ts, ns.stt, ns.recip, ns.memset)
    mix, dr, h1s, h1T, y = ns.mix, ns.dr, ns.h1s, ns.h1T, ns.y
    ident, epsc = ns.ident, ns.epsc
    base = A.mark()

    def layer_norm(src, dst, gam, bet, tag, gk, bk):
        stats, mv, rstd = ln_tmp
        for half in range(2):
            P.add("dve", lambda e, half=half: e.bn_stats(out=stats[:, half, :], in_=src[:, half * 512:(half + 1) * 512]),
                  r=[tag + "_src"], w=["ln_stats"])
        P.add("dve", lambda e: e.bn_aggr(out=mv, in_=stats.rearrange("p a b -> p (a b)")), r=["ln_stats"], w=["ln_mv"])
        act(rstd, mv[:, 1:2], AF.Sqrt, r=["ln_mv"], w=["ln_rstd"], bias=epsc, scale=1.0)
        recip(rstd, rstd, r=["ln_rstd"], w=["ln_rstd"])
        ts("dve", src, src, mv[:, 0:1], rstd, ALU.subtract, ALU.mult, r=[tag + "_src", "ln_mv", "ln_rstd"], w=[tag + "_src"])
        tt("pool", dst, src, gam, ALU.mult, r=[tag + "_src", gk], w=[tag + "_dst"])
        tt("pool", dst, dst, bet, ALU.add, r=[tag + "_dst", bk], w=[tag + "_dst"])

    w_out_bf = A.alloc([8, 1024], BF16)
    wst = A.alloc([2816], F32)
    g1 = A.alloc([1024], F32)
    b1 = A.alloc([1024], F32)
    mixt = [A.alloc([1024], BF16) for _ in range(2)]
    xt = [A.alloc([1024], F32) for _ in range(2)]
    mixT = A.alloc([8, 128], BF16)
    hpre = A.alloc([1024], F32)
    h1 = [A.alloc([1024], F32) for _ in range(2)]
    h1b = A.alloc([1024], BF16)
    h1Tt = [A.alloc([8, 128], BF16) for _ in range(2)]
    ln_tmp = (A.alloc([2, 6], F32), A.alloc([2], F32), A.alloc([1], F32))
    for fc in range(8):
        P.dma(wst[:, 0:1024], dr["w_out"][fc * 128:(fc + 1) * 128, :], w=["wst"])
        cp("dve" if fc % 2 == 0 else "pool", w_out_bf[:, fc, :], wst[:, 0:1024], r=["wst"], w=["woutb"])
    P.dma(g1, dr["ln1_g"][0:1, :].to_broadcast([128, 1024]), w=["g1"])
    P.dma(b1, dr["ln1_b"][0:1, :].to_broadcast([128, 1024]), w=["b1"])
    h1T_v = h1T.rearrange("f p t -> p f t")
    for t_i in range(NT):
        sl = t_i % 2
        P.dma(mixt[sl], mix[t_i * 128:(t_i + 1) * 128, :], w=["mixt%d" % sl])
        P.dma(xt[sl], dr["x"][t_i * 128:(t_i + 1) * 128, :], w=["xt%d" % sl])
        for fc in range(8):
            tr(bank_bf(7)[:, fc * 128:(fc + 1) * 128], mixt[sl][:, fc * 128:(fc + 1) * 128], ident,
               r=["mixt%d" % sl], w=["b7"])
        cp("act", mixT, bank_bf(7).rearrange("p (a b) -> p a b", b=128), r=["b7"], w=["mixT"])
        for half in range(2):
            for fc in range(8):
                mm(bank(4 + half), mixT[:, fc, :], w_out_bf[:, fc, half * 512:(half + 1) * 512], fc == 0, fc == 7,
                   r=["mixT", "woutb"], w=["b%d" % (4 + half)])
            stt(hpre[:, half * 512:(half + 1) * 512], xt[sl][:, half * 512:(half + 1) * 512], ALPHA, bank(4 + half),
                ALU.mult, ALU.add, r=["xt%d" % sl, "b%d" % (4 + half)], w=["l1_src"])
        P.ln_args = None
        layer_norm(hpre, h1[sl], g1, b1, "l1", "g1", "b1")
        P.dma(h1s[t_i * 128:(t_i + 1) * 128, :], h1[sl], r=["l1_dst"], w=["h1s"])
        cp("act", h1b, h1[sl], r=["l1_dst"], w=["h1b"])
        for fc in range(8):
            tr(bank_bf(6)[:, fc * 128:(fc + 1) * 128], h1b[:, fc * 128:(fc + 1) * 128], ident, r=["h1b"], w=["b6"])
        cp("dve", h1Tt[sl], bank_bf(6).rearrange("p (a b) -> p a b", b=128), r=["b6"], w=["h1Tt%d" % sl])
        P.dma(h1T_v[:, :, t_i * 128:(t_i + 1) * 128], h1Tt[sl], r=["h1Tt%d" % sl], w=["h1T"])
    P.barrier()
    A.reset(base)

    NC5 = S // 256
    w_up_bf = A.alloc([8, 2 * D_FF], BF16)
    w_dn_bf = A.alloc([22, 1024], BF16)
    wst = A.alloc([2816], F32)
    g2 = A.alloc([1024], F32)
    b2 = A.alloc([1024], F32)
    identf = A.alloc([128], F32)
    cwst = A.alloc([4, 128], F32)
    cwT = A.alloc([4, 44], F32)
    h1Tc = [A.alloc([8, 258], BF16) for _ in range(2)]
    Aa = [A.alloc([256], F32) for _ in range(2)]
    bb = [A.alloc([256], F32) for _ in range(2)]
    cc = [A.alloc([256], F32) for _ in range(2)]
    sg = A.alloc([256], F32)
    G = [A.alloc([256], BF16) for _ in range(2)]
    h1r = [A.alloc([1024], F32) for _ in range(2)]
    hpre2 = A.alloc([1024], F32)
    outt = [A.alloc([1024], F32) for _ in range(2)]
    ln_tmp = (A.alloc([2, 6], F32), A.alloc([2], F32), A.alloc([1], F32))
    engs3 = ["dve", "pool", "act"]
    k = 0
    for fc in range(8):
        for half in range(2):
            P.dma(wst, dr["w_up"][fc * 128:(fc + 1) * 128, half * 2816:(half + 1) * 2816], w=["wst"])
            for j3 in range(3):
                c0 = half * 2816 + j3 * 940
                c1 = half * 2816 + min(2816, (j3 + 1) * 940) if j3 < 2 else (half + 1) * 2816
                cp(engs3[j3], w_up_bf[:, fc, c0:c1], wst[:, c0 - half * 2816:c1 - half * 2816], r=["wst"], w=["wupb%d" % j3])
    for i2 in range(11):
        P.dma(wst[:, 0:2048].rearrange("p (a b) -> p a b", b=1024),
              dr["w_down"][i2 * 256:(i2 + 1) * 256, :].rearrange("(a p) c -> p a c", p=128), w=["wst"])
        cp("dve", w_dn_bf[:, 2 * i2, :], wst[:, 0:1024], r=["wst"], w=["wdnb0"])
        cp("pool", w_dn_bf[:, 2 * i2 + 1, :], wst[:, 1024:2048], r=["wst"], w=["wdnb1"])
    P.dma(g2, dr["ln2_g"][0:1, :].to_broadcast([128, 1024]), w=["g2"])
    P.dma(b2, dr["ln2_b"][0:1, :].to_broadcast([128, 1024]), w=["b2"])
    cp("dve", identf, ident, r=[], w=["identf"])
    for kk in range(3):
        P.dma(cwst[0:44, kk, :], dr["conv_w"][kk:kk + 1, :].rearrange("o (c p) -> (o c) p", p=128), w=["cwst"])
    P.dma(cwst[0:44, 3, :], dr["conv_b"][0:1, :].rearrange("o (c p) -> (o c) p", p=128), w=["cwst"])
    for kk in range(4):
        tr(bank(7, kk * 44, 44), cwst[0:44, kk, :], identf[0:44, 0:44], r=["cwst", "identf"], w=["b7"])
    cp("dve", cwT, bank(7, 0, 176).rearrange("p (a b) -> p a b", b=44), r=["b7"], w=["cwT"])
    WK = ["wupb0", "wupb1", "wupb2"]
    for c in range(NC5):
        sl = c % 2
        if c == 0:
            memset("pool", h1Tc[sl][:, :, 0:2], 0.0, w=["h1Tc%d" % sl])
            P.dma(h1Tc[sl][:, :, 2:258], h1T_v[:, :, 0:256], w=["h1Tc%d" % sl])
        else:
            P.dma(h1Tc[sl], h1T_v[:, :, c * 256 - 2:c * 256 + 256], w=["h1Tc%d" % sl])
        def ffn_up(i, sl=sl):
            for which in range(2):
                col0 = which * 2816 + i * 128
                cid = which * 22 + i
                bn = 4 + 2 * (i % 2) + which
                pb = bank(bn)
                for fc in range(8):
                    mm(pb[:, 0:258], w_up_bf[:, fc, col0:col0 + 128], h1Tc[sl][:, fc, :], fc == 0, fc == 7,
                       r=["h1Tc%d" % sl] + WK, w=["b%d" % bn])
                act(Aa[which], pb[:, 2:258], AF.Identity, r=["b%d" % bn, "cwT"], w=["Aa%d" % which],
                    scale=cwT[:, 2, cid:cid + 1], bias=cwT[:, 3, cid:cid + 1])
                stt(bb[which], pb[:, 1:257], cwT[:, 1, cid:cid + 1], Aa[which], ALU.mult, ALU.add,
                    r=["b%d" % bn, "Aa%d" % which, "cwT"], w=["bb%d" % which])
                stt(cc[which], pb[:, 0:256], cwT[:, 0, cid:cid + 1], bb[which], ALU.mult, ALU.add,
                    r=["b%d" % bn, "bb%d" % which, "cwT"], w=["cc%d" % which])
            act(sg, cc[0], AF.Silu, r=["cc0"], w=["sg"])
            tt("pool", G[i % 2], sg, cc[1], ALU.mult, r=["sg", "cc1"], w=["G%d" % (i % 2)])

        def ffn_down(i):
            for sub in range(2):
                for half in range(2):
                    mm(bank(sub * 2 + half), G[i % 2][:, sub * 128:(sub + 1) * 128],
                       w_dn_bf[:, i, half * 512:(half + 1) * 512], i == 0, i == 21,
                       r=["G%d" % (i % 2), "wdnb0", "wdnb1"], w=["b%d" % (sub * 2 + half)])

        for i in range(23):
            if i < 22:
                ffn_up(i)
            if i >= 1:
                ffn_down(i - 1)
        for sub in range(2):
            t_i = c * 2 + sub
            P.dma(h1r[sub], h1s[t_i * 128:(t_i + 1) * 128, :], w=["h1r%d" % sub])
            for half in range(2):
                stt(hpre2[:, half * 512:(half + 1) * 512], h1r[sub][:, half * 512:(half + 1) * 512], ALPHA,
                    bank(sub * 2 + half), ALU.mult, ALU.add, r=["h1r%d" % sub, "b%d" % (sub * 2 + half)], w=["l2_src"])
            layer_norm(hpre2, outt[sub], g2, b2, "l2", "g2", "b2")
            P.dma(y[t_i * 128:(t_i + 1) * 128, :], outt[sub], r=["l2_dst"], w=["y"])


_NC_CACHE = {}


def kernel(**inputs):
    S = int(np.asarray(inputs["x"]).shape[1])
    B = int(np.asarray(inputs["x"]).shape[0])
    if S not in _NC_CACHE:
        _NC_CACHE[S] = build(S)
    nc = _NC_CACHE[S]
    in_maps = [core_inputs(inputs, b, S) for b in range(B)]
    res = run_bass_kernel_spmd(nc, in_maps, core_ids=list(range(B)))
    out = np.stack([np.asarray(res.results[b]["y"], dtype=np.float32).reshape(S, 1024) for b in range(B)], axis=0)
    return out
```
